# Optimizing a Trainium2 kernel written in Bass

```python
import math
import jax, jax.numpy as jnp
from jax import lax
import numpy as np

D_MODEL = 1024
BATCH = 2
SEQ = 8192
DEPTH = 1
DEC_BATCH = 2
DEC_SEQ = 16384
PAST_LEN = 128

N_META = 16
MLA_H = 8
NOPE_DIM = 128
ROPE_DIM = 64
QK_DIM = NOPE_DIM + ROPE_DIM
V_DIM = 128
Q_LORA = 768
KV_LORA = 256
ROPE_THETA = 10000.0
Q_BLOCK = 128
GLA_H = 4
GLA_DK = 128
GLA_DV = 256
GATE_RANK = 16
GATE_TEMP = 16.0
GLA_CHUNK = 64
META_PAD = GLA_CHUNK - N_META
D_FF = 2816
NORM_EPS = 1e-6

IN_SIZES = (Q_LORA, KV_LORA, ROPE_DIM,
            GLA_H * GLA_DK, GLA_H * GLA_DK, GLA_H * GLA_DV, GLA_H * GLA_DV,
            GATE_RANK, GATE_RANK,
            D_MODEL, D_MODEL)
D_IN = Q_LORA + KV_LORA + ROPE_DIM + 2 * GLA_H * GLA_DK + 2 * GLA_H * GLA_DV + 2 * GATE_RANK + 2 * D_MODEL

kernel_name = "hybrid_mla_gla_gated_encoder"


def rmsnorm(x, g):
    xf = x.astype(jnp.float32)
    y = xf * lax.rsqrt(jnp.mean(xf * xf, axis=-1, keepdims=True) + NORM_EPS)
    return (y * g.astype(jnp.float32)).astype(x.dtype)


def split_cols(z, sizes):
    out = []
    off = 0
    for s in sizes:
        out.append(z[..., off:off + s])
        off += s
    return out


def rope_tables(length):
    inv = 1.0 / (ROPE_THETA ** (jnp.arange(0, ROPE_DIM, 2, dtype=jnp.float32) / ROPE_DIM))
    ang = jnp.arange(length, dtype=jnp.float32)[:, None] * inv[None, :]
    return jnp.cos(ang), jnp.sin(ang)


def apply_rope(x, cos, sin):
    xf = x.astype(jnp.float32)
    half = ROPE_DIM // 2
    x1, x2 = xf[..., :half], xf[..., half:]
    c = cos[None, :, None, :]
    s = sin[None, :, None, :]
    return jnp.concatenate([x1 * c - x2 * s, x2 * c + x1 * s], axis=-1).astype(x.dtype)


def block_attention(q, k, v):
    b, length, h, d = q.shape
    n_blk = -(-length // Q_BLOCK)
    pad = n_blk * Q_BLOCK - length
    qp = jnp.pad(q, ((0, 0), (0, pad), (0, 0), (0, 0)))
    qp = qp.reshape(b, n_blk, Q_BLOCK, h, d).transpose(1, 0, 2, 3, 4)
    scale = QK_DIM ** -0.5

    def one_block(qb):
        s = jnp.einsum('bqhd,bkhd->bhqk', qb, k, preferred_element_type=jnp.float32) * scale
        p = jax.nn.softmax(s, axis=-1)
        return jnp.einsum('bhqk,bkhv->bqhv', p.astype(v.dtype), v)

    o = lax.map(one_block, qp)
    o = o.transpose(1, 0, 2, 3, 4).reshape(b, n_blk * Q_BLOCK, h, v.shape[-1])
    return o[:, :length]


def mla_branch(c_q, c_kv, k_rope, q_a_norm, w_uq, kv_a_norm, w_ukv, q_norm, k_norm, w_o, cos, sin):
    b, length, _ = c_q.shape
    q = (rmsnorm(c_q, q_a_norm) @ w_uq).reshape(b, length, MLA_H, QK_DIM)
    kv = (rmsnorm(c_kv, kv_a_norm) @ w_ukv).reshape(b, length, MLA_H, NOPE_DIM + V_DIM)
    k_nope, v = kv[..., :NOPE_DIM], kv[..., NOPE_DIM:]
    k_r = jnp.broadcast_to(k_rope[:, :, None, :], (b, length, MLA_H, ROPE_DIM))
    k = jnp.concatenate([k_nope, k_r], axis=-1)
    q = rmsnorm(q, q_norm)
    k = rmsnorm(k, k_norm)
    q = jnp.concatenate([q[..., :NOPE_DIM], apply_rope(q[..., NOPE_DIM:], cos, sin)], axis=-1)
    k = jnp.concatenate([k[..., :NOPE_DIM], apply_rope(k[..., NOPE_DIM:], cos, sin)], axis=-1)
    o = block_attention(q, k, v)
    return o.reshape(b, length, MLA_H * V_DIM) @ w_o


def gla_chunked(q, k, v, log_g):
    b, t, h, dk = q.shape
    dv = v.shape[-1]
    n = t // GLA_CHUNK
    f32 = jnp.float32
    qc = q.astype(f32).reshape(b, n, GLA_CHUNK, h, dk)
    kc = k.astype(f32).reshape(b, n, GLA_CHUNK, h, dk)
    vc = v.astype(f32).reshape(b, n, GLA_CHUNK, h, dv)
    bcum = jnp.cumsum(log_g.astype(f32).reshape(b, n, GLA_CHUNK, h, dk), axis=2)
    b_last = bcum[:, :, -1:]
    b_ref = bcum[:, :, GLA_CHUNK // 2 - 1:GLA_CHUNK // 2]
    a = jnp.einsum('bnihd,bnjhd->bnhij', qc * jnp.exp(bcum - b_ref), kc * jnp.exp(b_ref - bcum))
    causal = jnp.tril(jnp.ones((GLA_CHUNK, GLA_CHUNK), dtype=bool))
    a = jnp.where(causal, a, 0.0)
    o_intra = jnp.einsum('bnhij,bnjhv->bnihv', a, vc)
    u = jnp.einsum('bnchd,bnchv->nbhdv', kc * jnp.exp(b_last - bcum), vc)
    decay = jnp.exp(b_last[:, :, 0]).transpose(1, 0, 2, 3)

    def step(s, inp):
        d, u_n = inp
        return d[..., None] * s + u_n, s

    _, s_prev = lax.scan(step, jnp.zeros((b, h, dk, dv), f32), (decay, u))
    o_inter = jnp.einsum('bnchd,nbhdv->bnchv', qc * jnp.exp(bcum), s_prev)
    return (o_intra + o_inter).reshape(b, t, h, dv).astype(v.dtype)


def gla_branch(q, k, v, g, a_f, a_b, w_a2_f, b_a2_f, w_a2_b, b_a2_b, o_norm, w_o):
    b, length, _ = q.shape
    q = q.reshape(b, length, GLA_H, GLA_DK) * (GLA_DK ** -0.5)
    k = k.reshape(b, length, GLA_H, GLA_DK)
    v = v.reshape(b, length, GLA_H, GLA_DV)
    lg_f = (jax.nn.log_sigmoid((a_f @ w_a2_f + b_a2_f).astype(jnp.float32)) / GATE_TEMP).reshape(b, length, GLA_H, GLA_DK)
    lg_b = (jax.nn.log_sigmoid((a_b @ w_a2_b + b_a2_b).astype(jnp.float32)) / GATE_TEMP).reshape(b, length, GLA_H, GLA_DK)
    pad = ((0, 0), (META_PAD, 0), (0, 0), (0, 0))
    qp, kp, vp = jnp.pad(q, pad), jnp.pad(k, pad), jnp.pad(v, pad)
    lfp, lbp = jnp.pad(lg_f, pad), jnp.pad(lg_b, pad)
    o_f = gla_chunked(qp, kp, vp, lfp)
    o_b = jnp.flip(gla_chunked(jnp.flip(qp, 1), jnp.flip(kp, 1), jnp.flip(vp, 1), jnp.flip(lbp, 1)), 1)
    o = (o_f + o_b)[:, META_PAD:]
    o = rmsnorm(o, o_norm) * jax.nn.silu(g).reshape(b, length, GLA_H, GLA_DV)
    return o.reshape(b, length, GLA_H * GLA_DV) @ w_o


def encoder_layer(x, attn_norm, w_in, q_a_norm, w_uq, kv_a_norm, w_ukv, q_norm, k_norm, w_o_mla,
                  w_a2_fwd, b_a2_fwd, w_a2_bwd, b_a2_bwd, gla_o_norm, w_o_gla, w_out,
                  ffn_norm, w_ffn_gate, w_ffn_up, w_ffn_down):
    length = x.shape[1]
    h = rmsnorm(x, attn_norm)
    z = h @ w_in
    (c_q, c_kv, k_rope, gq, gk, gv, gg, a_f, a_b, gate_a, gate_b) = split_cols(z, IN_SIZES)
    cos, sin = rope_tables(length)
    y_a = mla_branch(c_q, c_kv, k_rope, q_a_norm, w_uq, kv_a_norm, w_ukv, q_norm, k_norm, w_o_mla, cos, sin)
    y_b = gla_branch(gq, gk, gv, gg, a_f, a_b, w_a2_fwd, b_a2_fwd, w_a2_bwd, b_a2_bwd, gla_o_norm, w_o_gla)
    mixed = (jax.nn.sigmoid(gate_a) * y_a + jax.nn.sigmoid(gate_b) * y_b) @ w_out
    x = x + mixed
    h = rmsnorm(x, ffn_norm)
    x = x + (jax.nn.silu(h @ w_ffn_gate) * (h @ w_ffn_up)) @ w_ffn_down
    return x


def setup_inputs(seed: int = 0) -> dict:
    key = jax.random.key(seed)
    ks = jax.random.split(key, 24)
    f32 = jnp.float32

    def nrm(k, shape, scale):
        return jax.random.normal(k, shape, f32) * scale

    def gain(k, n):
        return 1.0 + 0.01 * jax.random.normal(k, (DEPTH, n), f32)

    return {
        "x_prompt": nrm(ks[0], (BATCH, SEQ, D_MODEL), 1.0),
        "x_sample": nrm(ks[1], (DEC_BATCH, DEC_SEQ, D_MODEL), 1.0),
        "meta_tokens": nrm(ks[2], (N_META, D_MODEL), 1.0),
        "attn_norm": gain(ks[3], D_MODEL),
        "w_in": nrm(ks[4], (DEPTH, D_MODEL, D_IN), D_MODEL ** -0.5),
        "q_a_norm": gain(ks[5], Q_LORA),
        "w_uq": nrm(ks[6], (DEPTH, Q_LORA, MLA_H * QK_DIM), Q_LORA ** -0.5),
        "kv_a_norm": gain(ks[7], KV_LORA),
        "w_ukv": nrm(ks[8], (DEPTH, KV_LORA, MLA_H * (NOPE_DIM + V_DIM)), KV_LORA ** -0.5),
        "q_norm": gain(ks[9], QK_DIM),
        "k_norm": gain(ks[10], QK_DIM),
        "w_o_mla": nrm(ks[11], (DEPTH, MLA_H * V_DIM, D_MODEL), (MLA_H * V_DIM) ** -0.5),
        "w_a2_fwd": nrm(ks[12], (DEPTH, GATE_RANK, GLA_H * GLA_DK), GATE_RANK ** -0.5),
        "b_a2_fwd": nrm(ks[13], (DEPTH, GLA_H * GLA_DK), 0.1),
        "w_a2_bwd": nrm(ks[14], (DEPTH, GATE_RANK, GLA_H * GLA_DK), GATE_RANK ** -0.5),
        "b_a2_bwd": nrm(ks[15], (DEPTH, GLA_H * GLA_DK), 0.1),
        "gla_o_norm": gain(ks[16], GLA_DV),
        "w_o_gla": nrm(ks[17], (DEPTH, GLA_H * GLA_DV, D_MODEL), (GLA_H * GLA_DV) ** -0.5),
        "w_out": nrm(ks[18], (DEPTH, D_MODEL, D_MODEL), D_MODEL ** -0.5),
        "ffn_norm": gain(ks[19], D_MODEL),
        "w_ffn_gate": nrm(ks[20], (DEPTH, D_MODEL, D_FF), D_MODEL ** -0.5),
        "w_ffn_up": nrm(ks[21], (DEPTH, D_MODEL, D_FF), D_MODEL ** -0.5),
        "w_ffn_down": nrm(ks[22], (DEPTH, D_FF, D_MODEL), D_FF ** -0.5),
    }


def reference(x_prompt, x_sample, meta_tokens, attn_norm, w_in, q_a_norm, w_uq, kv_a_norm, w_ukv,
              q_norm, k_norm, w_o_mla, w_a2_fwd, b_a2_fwd, w_a2_bwd, b_a2_bwd, gla_o_norm, w_o_gla,
              w_out, ffn_norm, w_ffn_gate, w_ffn_up, w_ffn_down):
    def encoder(x):
        b = x.shape[0]
        meta = jnp.broadcast_to(meta_tokens[None].astype(x.dtype), (b, N_META, D_MODEL))
        x = jnp.concatenate([meta, x], axis=1)
        for l in range(DEPTH):
            x = encoder_layer(x, attn_norm[l], w_in[l], q_a_norm[l], w_uq[l], kv_a_norm[l], w_ukv[l],
                              q_norm[l], k_norm[l], w_o_mla[l], w_a2_fwd[l], b_a2_fwd[l], w_a2_bwd[l],
                              b_a2_bwd[l], gla_o_norm[l], w_o_gla[l], w_out[l], ffn_norm[l],
                              w_ffn_gate[l], w_ffn_up[l], w_ffn_down[l])
        return x[:, N_META:]

    y_prompt = encoder(x_prompt)
    y_sample = encoder(x_sample)
    return (y_prompt, y_sample)
```

```python
import contextlib
import numpy as np
import concourse.bass as bass
import concourse.mybir as mybir
from concourse.bass_utils import run_bass_kernel_spmd

F32 = mybir.dt.float32
BF16 = mybir.dt.bfloat16
AF = mybir.ActivationFunctionType
ALU = mybir.AluOpType
AX = mybir.AxisListType

D = 1024
NM = 16
H = 8
NOPE = 128
ROPE = 64
QK = 192
VD = 128
QL = 768
KVL = 256
GH = 4
GDK = 128
GDV = 256
DFF = 2816
EPS = 1e-6
NCTX = 1856
NMAIN = 5920
N_DMA_SEMS = 10
import os
STQ = os.environ.get('STQ', 'pool')
CUT = int(os.environ.get('CUT', '99'))
TLIM = int(os.environ.get('TLIM', '100000'))


class Prog:
    ENGS = ("pe", "act", "dve", "pool", "sp")
    EPOCH = 30000

    G = None

    def __init__(self, nc):
        self.nc = nc
        self.streams = {e: [] for e in self.ENGS}
        g = Prog.G
        self.cnt = g["cnt"]
        self.sems = g["sems"]
        self.epoch = g["epoch"]
        self.known = {e: dict(self.cnt) for e in self.ENGS}
        self.last_w = {}
        self.readers = {}
        self.dma_rr = 0

    def _pe(self, eng):
        ep = self.epoch.setdefault(eng, 0)
        name = "%s_%d" % (eng, ep)
        if self.cnt.get(name, 0) >= self.EPOCH:
            self.epoch[eng] = ep + 1
            name = "%s_%d" % (eng, ep + 1)
        return name

    def _deps(self, reads, writes):
        need = {}

        def add(pe, v):
            if need.get(pe, 0) < v:
                need[pe] = v

        for k in reads:
            for pe, v in self.last_w.get(k, {}).items():
                add(pe, v)
        for k in writes:
            for pe, v in self.last_w.get(k, {}).items():
                add(pe, v)
            for pe, v in self.readers.get(k, {}).items():
                add(pe, v)
        return need

    def _filter(self, eng, need):
        waits = []
        kn = self.known[eng]
        for pe, v in need.items():
            if eng == "pe" and pe.startswith("pe_"):
                continue
            if kn.get(pe, 0) >= v:
                continue
            kn[pe] = v
            waits.append((pe, v))
        return waits

    def _record(self, reads, writes, pe, v):
        for k in reads:
            self.readers.setdefault(k, {})[pe] = v
        for k in writes:
            self.last_w.setdefault(k, {})[pe] = v

    def op(self, eng, fn, reads=(), writes=(), inc=True):
        if eng != "pe":
            writes = list(writes) + [k for k in reads if k.startswith("pb")]
        need = self._deps(reads, writes)
        waits = self._filter(eng, need)
        pe = self._pe(eng)
        v = self.cnt.get(pe, 0) + 1
        if inc:
            self.cnt[pe] = v
            self.streams[eng].append((waits, fn, (pe, 1)))
        else:
            assert eng == "pe"
            self.streams[eng].append((waits, fn, None))
        self._record(reads, writes, pe, v)

    def dma(self, fn, reads=(), writes=(), q="sp"):
        s = self.dma_rr
        self.dma_rr = (self.dma_rr + 1) % N_DMA_SEMS
        ep = self.epoch.setdefault("q%d" % s, 0)
        if self.cnt.get("q%d_%d" % (s, ep), 0) >= 1800:
            ep += 1
            self.epoch["q%d" % s] = ep
        pe = "q%d_%d" % (s, ep)
        need = self._deps(reads, writes)
        prev = self.cnt.get(pe, 0)
        if prev:
            need[pe] = max(need.get(pe, 0), prev)
        waits = self._filter(q, need)
        v = prev + 1
        self.cnt[pe] = v
        self.streams[q].append((waits, fn, (pe, 16)))
        self._record(reads, writes, pe, v)

    def build(self):
        nc = self.nc
        for e in self.ENGS:
            waits = self._filter(e, dict(self.cnt))
            if waits:
                self.streams[e].append((waits, None, None))
        with contextlib.ExitStack() as st:
            for n in list(self.cnt.keys()):
                if n not in self.sems:
                    self.sems[n] = Prog.G["st"].enter_context(nc.semaphore("s_" + n))
            block = st.enter_context(nc.Block())

            def run(engh, stream):
                for waits, fn, inc in stream:
                    for pe, v in waits:
                        engh.wait_ge(self.sems[pe], v * (16 if pe.startswith("q") else 1))
                    if fn is None:
                        continue
                    ins = fn(engh)
                    if inc is not None:
                        ins.then_inc(self.sems[inc[0]], inc[1])

            @block.sync
            def _(e):
                run(e, self.streams["sp"])

            @block.tensor
            def _(e):
                run(e, self.streams["pe"])

            @block.scalar
            def _(e):
                run(e, self.streams["act"])

            @block.vector
            def _(e):
                run(e, self.streams["dve"])

            @block.gpsimd
            def _(e):
                run(e, self.streams["pool"])


class Ctx:
    PH = [0]

    def __init__(self, nc, st):
        self.nc = nc
        self.st = st
        Ctx.PH[0] += 1
        self.pfx = "ph%d_" % Ctx.PH[0]
        self.P = Prog(nc)
        self.P.pfx = self.pfx
        self.banks = [st.enter_context(nc.psum_tensor(self.pfx + "pb%d" % i, [128, 512], F32)) for i in range(8)]
        self.bank_rr = 0
        self.nrot = 8
        self.uid = 0

    def sb(self, name, shape, dt):
        return self.st.enter_context(self.nc.sbuf_tensor(self.pfx + name, shape, dt))

    def bank(self):
        i = self.bank_rr
        self.bank_rr = (i + 1) % self.nrot
        return self.banks[i], "pb%d" % i

    def act(self, out, in_, func, r, w, scale=1.0, bias=0.0, accum=None):
        if accum is None:
            self.P.op("act", lambda e: e.activation(out=out, in_=in_, func=func, scale=scale, bias=bias), r, w)
        else:
            self.P.op("act", lambda e: e.activation(out=out, in_=in_, func=func, scale=scale, bias=bias,
                                                    accum_out=accum), r, w)

    def tt(self, out, in0, in1, op, r, w, eng="dve"):
        self.P.op(eng, lambda e: e.tensor_tensor(out=out, in0=in0, in1=in1, op=op), r, w)

    def ts(self, out, in0, s1, op0, r, w, s2=None, op1=None, eng="dve"):
        if op1 is None:
            self.P.op(eng, lambda e: e.tensor_scalar(out=out, in0=in0, scalar1=s1, scalar2=None, op0=op0), r, w)
        else:
            self.P.op(eng, lambda e: e.tensor_scalar(out=out, in0=in0, scalar1=s1, scalar2=s2, op0=op0, op1=op1), r, w)

    def stt(self, out, in0, scalar, in1, op0, op1, r, w):
        self.P.op("dve", lambda e: e.scalar_tensor_tensor(out=out, in0=in0, scalar=scalar, in1=in1, op0=op0, op1=op1), r, w)

    def copy(self, out, in_, r, w, eng="dve"):
        if eng == "act":
            self.P.op("act", lambda e: e.copy(out=out, in_=in_), r, w)
        else:
            self.P.op(eng, lambda e: e.tensor_copy(out=out, in_=in_), r, w)

    def red(self, out, in_, r, w):
        self.P.op("dve", lambda e: e.tensor_reduce(out=out, in_=in_, axis=AX.X, op=ALU.add), r, w)

    def mm(self, out, lhsT, rhs, start, stop, r, w, inc=True, sgc=False):
        self.P.op("pe", lambda e: e.matmul(out, lhsT=lhsT, rhs=rhs, start=start, stop=stop, skip_group_check=sgc), r, w,
                  inc=inc)

    def tr(self, out, in_, ident, r, w, inc=True):
        self.P.op("pe", lambda e: e.transpose(out=out, in_=in_, identity=ident), r, w, inc=inc)

    def dma(self, out, in_, r=(), w=(), q="sp"):
        self.P.dma(lambda e: e.dma_start(out=out, in_=in_), r, w, q=q)

    def memset(self, ap, val, w, eng="pool"):
        self.P.op(eng, lambda e: e.memset(ap, val), (), w)

    def rstd(self, out, ss, inv_n, key_ss, key_out, epsb):
        self.act(out, ss, AF.Ln, [key_ss, "epsb"], [key_out], scale=inv_n, bias=epsb)
        self.act(out, out, AF.Exp, [key_out], [key_out], scale=-0.5)

    def load_consts(self, A):
        nc = self.nc
        self.identf = self.sb("identf", [128, 128], F32)
        self.ident = self.sb("ident", [128, 128], BF16)
        self.epsb = self.sb("epsb", [128, 1], F32)
        self.P.op("pool", lambda e: e.memset(self.identf[:], 0.0), (), ["identf"])
        self.P.op("pool", lambda e: e.affine_select(out=self.identf[:], in_=self.identf[:], pattern=[[-1, 128]],
                                                    compare_op=ALU.not_equal, fill=1.0, base=0,
                                                    channel_multiplier=1), ["identf"], ["identf"])
        self.copy(self.ident[:], self.identf[:], ["identf"], ["ident"])
        self.P.op("pool", lambda e: e.memset(self.epsb[:], EPS), (), ["epsb"])
        self.oneb = self.sb("oneb", [128, 1], F32)
        self.P.op("pool", lambda e: e.memset(self.oneb[:], 1.0), (), ["oneb"])

    def load_weight(self, name, dst, src, nk, ncols, gain=None, stg=None, col0=0, colgain=None):
        CH = 2048
        for c in range(nk):
            for o in range(0, ncols, CH):
                n = min(CH, ncols - o)
                i = self.uid % 2
                self.uid += 1
                s = stg[i]
                self.dma(s[:, 0:n], src[c * 128:(c + 1) * 128, o:o + n], w=["stg%d" % i])
                if gain is not None:
                    self.act(dst[:, c, col0 + o:col0 + o + n], s[:, 0:n], AF.Copy, ["stg%d" % i, "gains"], [name],
                             scale=gain[:, c:c + 1])
                else:
                    self.copy(dst[:, c, col0 + o:col0 + o + n], s[:, 0:n], ["stg%d" % i], [name], eng="act")

    def xnorm_T(self, xsrc, nt, par, hT_out, hT_key, nk=8, ssinv=1.0 / D):
        junk, ssx, hb = self.junk, self.ssx[par], self.hb[par]
        kx = xsrc[1]
        x = xsrc[0]
        ncol = nk * 128
        self.act(junk[:nt, 0:ncol], x, AF.Square, [kx], ["junk", "ssx%d" % par], accum=ssx[:nt, 0:1])
        self.rstd(ssx[:nt, 1:2], ssx[:nt, 0:1], ssinv, "ssx%d" % par, "rsx%d" % par, self.epsb[:nt, 0:1])
        self.act(hb[:nt, 0:ncol], x, AF.Copy, [kx, "rsx%d" % par], ["hb%d" % par], scale=ssx[:nt, 1:2])
        pb, pk = self.bank()
        pv = pb[:].bitcast(BF16)
        for c in range(nk):
            self.tr(pv[:, c * 128:c * 128 + nt], hb[:nt, c * 128:(c + 1) * 128], self.ident[:nt, :nt],
                    ["hb%d" % par, "ident"], [pk], inc=(c == nk - 1))
        self.copy(hT_out.rearrange("p c t -> p (c t)") if nt == 128 else hT_out,
                  pv[:, 0:nk * 128] if nt == 128 else pv[:, 0:nk * 128].rearrange("p (c t) -> p c t", t=128)[:, :, 0:nt],
                  [pk], [hT_key])


def build(TQ):
    gst = contextlib.ExitStack()
    with gst:
        return _build(TQ, gst)


def _build(TQ, gst):
    import os
    NPH = int(os.environ.get("NPHASE", "99"))
    nc = bass.Bass("TRN2", target_bir_lowering=False)
    NS = len(TQ)
    Prog.G = {"cnt": {}, "sems": {}, "epoch": {}, "st": gst}

    def din(name, shape, dt=F32):
        return nc.dram_tensor(name, list(shape), dt, kind="ExternalInput").ap()

    def dout(name, shape, dt=F32):
        return nc.dram_tensor(name, list(shape), dt, kind="ExternalOutput").ap()

    def dscr(name, shape, dt=F32):
        return nc.dram_tensor(name, list(shape), dt, kind="Internal").ap()

    A = {}
    for s in range(NS):
        T = TQ[s]
        A["xo%d" % s] = din("xo%d" % s, [T, D])
        A["xc%d" % s] = din("xc%d" % s, [4, T, D])
        A["rco%d" % s] = din("rco%d" % s, [T, 64])
        A["rso%d" % s] = din("rso%d" % s, [T, 64])
        A["rcc%d" % s] = din("rcc%d" % s, [4, T, 64])
        A["rsc%d" % s] = din("rsc%d" % s, [4, T, 64])
        A["y%d" % s] = dout("y%d" % s, [T, D])
        nblk = 4 * T // 128 + 1
        A["KTn%d" % s] = dscr("KTn%d" % s, [H, 128, nblk * 128], BF16)
        A["KTr%d" % s] = dscr("KTr%d" % s, [H, 64, nblk * 128], BF16)
        A["Vs%d" % s] = dscr("Vs%d" % s, [H, 128, nblk, 128], BF16)
        A["snap%d" % s] = dscr("snap%d" % s, [T // 128, 128, GH * GDV])
        A["QTn%d" % s] = dscr("QTn%d" % s, [H, 128, T], BF16)
        A["QTr%d" % s] = dscr("QTr%d" % s, [H, 64, T], BF16)
        A["OT%d" % s] = dscr("OT%d" % s, [H, 128, T], BF16)
        A["sf%d" % s] = dscr("sf%d" % s, [128, GH * GDV])
        A["x1_%d" % s] = dscr("x1_%d" % s, [T, D])
        A["siga%d" % s] = dscr("siga%d" % s, [T, D])
        A["ybg%d" % s] = dscr("ybg%d" % s, [T, D])
    for name, shape in [("meta", [NM, D]), ("rcm", [NM, 64]), ("rsm", [NM, 64]), ("flags", [5, 128, 2]),
                        ("wa_c", [5, D, 16]), ("wa2_c", [5, 17, 512]), ("wa2_m", [2, 17, 512]),
                        ("w_ctx", [D, NCTX]), ("w_main", [D, NMAIN]),
                        ("w_uq", [QL, H * QK]), ("w_ukv", [KVL, H * 256]), ("w_o_mla", [D, D]),
                        ("w_o_gla", [D, D]), ("w_out", [D, D]), ("w_ffn_gate", [D, DFF]), ("w_ffn_up", [D, DFF]),
                        ("w_ffn_down", [DFF, D]),
                        ("attn_norm", [128, 8]), ("q_a_norm", [128, 6]), ("kv_a_norm", [128, 2]), ("q_norm", [1, QK]),
                        ("k_norm", [1, QK]), ("gla_o_norm", [128, 2]), ("ffn_norm", [128, 8]),
                        ("cmat", [128, 5, 128]), ("crhs", [128, 4])]:
        A[name] = din(name, shape)

    def gate_path(C, nt, wa2_ap, wa2_key, par):
        ab, aT, lgn = C.ab[par], C.aT[par], C.lgn[par]
        pb, pk = C.bank()
        pv = pb[:].bitcast(BF16)
        C.tr(pv[0:16, 0:nt], ab[:nt, :], C.ident[:nt, :nt], ["ab%d" % par, "ident"], [pk])
        C.copy(aT[0:16, 0:nt], pv[0:16, 0:nt], [pk], ["aT%d" % par])
        pb2, pk2 = C.bank()
        C.mm(pb2[:nt, :], aT[0:17, 0:nt], wa2_ap, True, True, ["aT%d" % par, wa2_key], [pk2])
        C.act(lgn[:nt, :], pb2[:nt, :], AF.Exp, [pk2], ["lgn%d" % par], scale=-1.0)
        C.act(lgn[:nt, :], lgn[:nt, :], AF.Ln, ["lgn%d" % par, "oneb"], ["lgn%d" % par], scale=1.0, bias=C.oneb[:nt, 0:1])
        return lgn, "lgn%d" % par

    with contextlib.ExitStack() as st:
        C = Ctx(nc, st)
        C.load_consts(A)
        stg = [C.sb("stg%d" % i, [128, 2048], F32) for i in range(2)]
        gains = C.sb("gains", [128, 16], F32)
        C.dma(gains[:, 0:8], A["attn_norm"], w=["gains"])
        C.dma(gains[:, 8:10], A["kv_a_norm"], w=["gains"])
        wctx = C.sb("wctx", [128, 8, NCTX + 80], BF16)
        wukv = C.sb("wukv", [128, 2, 2048], BF16)
        C.load_weight("wctx", wctx, A["w_ctx"], 8, NCTX, gain=gains[:, 0:8], stg=stg)
        for sl in range(5):
            C.load_weight("wctx", wctx, A["wa_c"][sl], 8, 16, gain=gains[:, 0:8], stg=stg, col0=NCTX + 16 * sl)
        C.load_weight("wukv", wukv, A["w_ukv"], 2, 2048, gain=gains[:, 8:10], stg=stg)
        wa2f = C.sb("wa2f", [17, 5, 512], F32)
        wa2 = C.sb("wa2", [17, 5, 512], BF16)
        C.dma(wa2f[:], A["wa2_c"].rearrange("s k n -> k s n"), w=["wa2f"])
        C.copy(wa2[:], wa2f[:], ["wa2f"], ["wa2"])
        gk = C.sb("gk", [128, QK], F32)
        C.dma(gk[:], A["k_norm"].partition_broadcast(128), w=["gk"])
        cmat = C.sb("cmat", [128, 5, 128], F32)
        crhs = C.sb("crhs", [128, 4], F32)
        C.dma(cmat[:], A["cmat"], w=["cmat"])
        C.dma(crhs[:], A["crhs"], w=["crhs"])
        flg = C.sb("flg", [128, 5, 2], F32)
        nflg = C.sb("nflg", [128, 5, 2], F32)
        C.dma(flg[:], A["flags"].rearrange("s p t -> p s t"), w=["flg"])
        C.ts(nflg[:], flg[:], -1.0 / 16.0, ALU.mult, ["flg"], ["nflg"])
        C.junk = C.sb("junk", [128, 2048], F32)
        C.ssx = [C.sb("ssx%d" % i, [128, 4], F32) for i in range(2)]
        C.hb = [C.sb("hb%d" % i, [128, D], BF16) for i in range(2)]
        C.ab = [C.sb("ab%d" % i, [128, 16], BF16) for i in range(2)]
        C.aT = [C.sb("aT%d" % i, [32, 128], BF16) for i in range(2)]
        C.lgn = [C.sb("lgn%d" % i, [128, 512], F32) for i in range(2)]
        for i in range(2):
            C.memset(C.aT[i][:], 1.0, ["aT%d" % i])
        xt = [C.sb("xt%d" % i, [128, D], F32) for i in range(2)]
        hT = [C.sb("hT%d" % i, [128, 8, 128], BF16) for i in range(2)]
        rc = [C.sb("rc%d" % i, [128, 64], F32) for i in range(2)]
        rs = [C.sb("rs%d" % i, [128, 64], F32) for i in range(2)]
        ckb = [C.sb("ckb%d" % i, [128, 256], BF16) for i in range(2)]
        ckT = [C.sb("ckT%d" % i, [128, 2, 128], BF16) for i in range(2)]
        kfull = [C.sb("kfull%d" % i, [128, H, QK], F32) for i in range(2)]
        kn32 = C.sb("kn32", [128, H, QK], F32)
        kr32 = C.sb("kr32", [128, H, 64], F32)
        t1 = C.sb("t1", [128, H, 64], F32)
        t2 = C.sb("t2", [128, H, 64], F32)
        knb = [C.sb("knb%d" % i, [128, H, QK], BF16) for i in range(2)]
        vsb = [C.sb("vsb%d" % i, [128, H, 128], BF16) for i in range(2)]
        ssk = [C.sb("ssk%d" % i, [128, 2 * H + 2], F32) for i in range(2)]
        ktn = [C.sb("ktn%d" % i, [128, H, 128], BF16) for i in range(2)]
        ktr = [C.sb("ktr%d" % i, [64, H, 128], BF16) for i in range(2)]
        k32 = [C.sb("k32_%d" % i, [128, 512], F32) for i in range(2)]
        gvb = [C.sb("gvb%d" % i, [128, 1024], BF16) for i in range(2)]
        Eb = [C.sb("Eb%d" % i, [128, 512], F32) for i in range(2)]
        kef = [C.sb("kef%d" % i, [128, 512], BF16) for i in range(2)]
        keb = [C.sb("keb%d" % i, [128, 512], BF16) for i in range(2)]
        dec = [C.sb("dec%d" % i, [128, 2, GH], F32) for i in range(2)]
        Sf = C.sb("Sf", [128, GH, GDV], F32)
        Sb = C.sb("Sb", [128, GH, GDV], F32)
        snapb = [C.sb("snapb%d" % i, [128, GH, GDV], F32) for i in range(2)]

        tile_no = [0]

        def ctx_tile(s, xsrc, rcsrc, rssrc, nt, sl, keyblk, snap_idx):
            par = tile_no[0] % 2
            tile_no[0] += 1
            p = "%d" % par
            C.dma(xt[par][:nt, :], xsrc, w=["xt" + p])
            C.dma(rc[par][:nt, :], rcsrc, w=["rc" + p])
            C.dma(rs[par][:nt, :], rssrc, w=["rs" + p])
            C.xnorm_T((xt[par][:nt, :], "xt" + p), nt, par, hT[par][:, :, 0:nt], "hT" + p)
            if CUT <= 1:
                return
            groups = [(0, 320), (320, 512), (832, 512), (1344, 512), (NCTX + 16 * sl, 16)]
            zb = []
            for (c0, n) in groups:
                pb, pk = C.bank()
                for c in range(8):
                    C.mm(pb[:nt, 0:n], hT[par][:, c, 0:nt], wctx[:, c, c0:c0 + n], c == 0, c == 7,
                         ["hT" + p, "wctx"], [pk], inc=(c == 7))
                zb.append((pb, pk))
            (pkv, kkv), (pgk, kgk), (pv0, kv0), (pv1, kv1), (pa, ka) = zb
            C.copy(k32[par][:nt, :], pgk[:nt, :], [kgk], ["k32_" + p], eng="act")
            C.copy(gvb[par][:nt, 0:512], pv0[:nt, :], [kv0], ["gvb" + p], eng="act")
            C.copy(gvb[par][:nt, 512:1024], pv1[:nt, :], [kv1], ["gvb" + p], eng="dve")
            C.copy(C.ab[par][:nt, :], pa[:nt, 0:16], [ka], ["ab" + p])
            if CUT <= 2:
                return
            C.act(C.junk[:nt, 0:256], pkv[:nt, 0:256], AF.Square, [kkv], ["junk", "ssk" + p],
                  accum=ssk[par][:nt, 16:17])
            C.rstd(ssk[par][:nt, 17:18], ssk[par][:nt, 16:17], 1.0 / KVL, "ssk" + p, "ssk" + p, C.epsb[:nt, 0:1])
            C.act(ckb[par][:nt, :], pkv[:nt, 0:256], AF.Copy, [kkv, "ssk" + p], ["ckb" + p], scale=ssk[par][:nt, 17:18])
            pb, pk = C.bank()
            pvw = pb[:].bitcast(BF16)
            for c in range(2):
                C.tr(pvw[:, c * 128:c * 128 + nt], ckb[par][:nt, c * 128:(c + 1) * 128], C.ident[:nt, :nt],
                     ["ckb" + p, "ident"], [pk], inc=(c == 1))
            C.copy(ckT[par][:, :, 0:nt], pvw[:, 0:256].rearrange("p (c t) -> p c t", t=128)[:, :, 0:nt], [pk], ["ckT" + p])
            C.copy(kfull[par][:nt, :, 128:192], pkv[:nt, 256:320].unsqueeze(1).to_broadcast([nt, H, 64]), [kkv],
                   ["kfull" + p])
            if CUT <= 3:
                return
            for g in range(4):
                pb, pk = C.bank()
                for c in range(2):
                    C.mm(pb[:nt, :], ckT[par][:, c, 0:nt], wukv[:, c, g * 512:(g + 1) * 512], c == 0, c == 1,
                         ["ckT" + p, "wukv"], [pk], inc=(c == 1))
                pv3 = pb[:nt, :].rearrange("p (h x) -> p h x", x=256)
                C.copy(kfull[par][:nt, 2 * g:2 * g + 2, 0:128], pv3[:, :, 0:128], [pk], ["kfull" + p], eng="act")
                C.copy(vsb[par][:nt, 2 * g:2 * g + 2, :], pv3[:, :, 128:256], [pk], ["vsb" + p], eng="dve")
            if CUT <= 4:
                return
            C.act(kn32[:nt], kfull[par][:nt], AF.Square, ["kfull" + p], ["kn32"])
            C.red(ssk[par][:nt, 0:H], kn32[:nt], ["kn32"], ["ssk" + p])
            C.rstd(ssk[par][:nt, H:2 * H], ssk[par][:nt, 0:H], 1.0 / QK, "ssk" + p, "ssk" + p, C.epsb[:nt, 0:1])
            C.tt(kn32[:nt], kfull[par][:nt], ssk[par][:nt, H:2 * H].unsqueeze(2).to_broadcast([nt, H, QK]), ALU.mult,
                 ["kfull" + p, "ssk" + p], ["kn32"])
            C.tt(knb[par][:nt, :, 0:128], kn32[:nt, :, 0:128], gk[:nt, 0:128].unsqueeze(1).to_broadcast([nt, H, 128]),
                 ALU.mult, ["kn32", "gk"], ["knb" + p])
            C.tt(kr32[:nt], kn32[:nt, :, 128:192], gk[:nt, 128:192].unsqueeze(1).to_broadcast([nt, H, 64]), ALU.mult,
                 ["kn32", "gk"], ["kr32"])
            C.tt(t1[:nt], kr32[:nt], rc[par][:nt, :].unsqueeze(1).to_broadcast([nt, H, 64]), ALU.mult,
                 ["kr32", "rc" + p], ["t1"])
            C.tt(t2[:nt, :, 0:32], kr32[:nt, :, 32:64], rs[par][:nt, 0:32].unsqueeze(1).to_broadcast([nt, H, 32]),
                 ALU.mult, ["kr32", "rs" + p], ["t2"])
            C.tt(t2[:nt, :, 32:64], kr32[:nt, :, 0:32], rs[par][:nt, 32:64].unsqueeze(1).to_broadcast([nt, H, 32]),
                 ALU.mult, ["kr32", "rs" + p], ["t2"])
            C.tt(knb[par][:nt, :, 128:192], t1[:nt], t2[:nt], ALU.add, ["t1", "t2"], ["knb" + p])
            if CUT <= 5:
                return
            pb, pk = C.bank()
            pvn = pb[:].bitcast(BF16)
            pb2, pk2 = C.bank()
            pvr = pb2[:].bitcast(BF16)
            for h in range(H):
                C.tr(pvn[:, h * 128:h * 128 + nt], knb[par][:nt, h, 0:128], C.ident[:nt, :nt], ["knb" + p, "ident"], [pk],
                     inc=False)
                C.tr(pvr[0:64, h * 128:h * 128 + nt], knb[par][:nt, h, 128:192], C.ident[:nt, :nt], ["knb" + p, "ident"],
                     [pk2], inc=(h == H - 1))
            C.copy(ktn[par][:, :, 0:nt], pvn[:, :].rearrange("p (h t) -> p h t", t=128)[:, :, 0:nt], [pk], ["ktn" + p],
                   eng="act")
            C.copy(ktr[par][:, :, 0:nt], pvr[0:64, :].rearrange("p (h t) -> p h t", t=128)[:, :, 0:nt], [pk2], ["ktr" + p],
                   eng="dve")
            k0 = keyblk * 128
            C.dma(A["KTn%d" % s][:, :, k0:k0 + nt].rearrange("h d t -> d h t"), ktn[par][:, :, 0:nt], r=["ktn" + p], q=STQ)
            C.dma(A["KTr%d" % s][:, :, k0:k0 + nt].rearrange("h d t -> d h t"), ktr[par][:, :, 0:nt], r=["ktr" + p], q=STQ)
            C.dma(A["Vs%d" % s][:, 0:nt, keyblk, :].rearrange("h p v -> p h v"), vsb[par][:nt], r=["vsb" + p], q=STQ)
            if CUT <= 6:
                return
            lgn, klg = gate_path(C, nt, wa2[0:17, sl, :], "wa2", par)
            pb, pk = C.bank()
            C.mm(pb[:nt, :], cmat[:nt, 0, 0:nt], lgn[:nt, :], True, True, ["cmat", klg], [pk])
            C.act(Eb[par][:nt, :], pb[:nt, :], AF.Exp, [pk], ["Eb" + p], scale=-1.0 / 16.0)
            fwd_only = (sl == 4)
            bwd_only = (sl == 3)
            if not bwd_only:
                C.stt(kef[par][:nt, :], k32[par][:nt, :], flg[:nt, sl, 0:1], Eb[par][:nt, :], ALU.mult, ALU.mult,
                      ["k32_" + p, "flg", "Eb" + p], ["kef" + p])
            if not fwd_only:
                C.stt(keb[par][:nt, :], k32[par][:nt, :], flg[:nt, sl, 1:2], Eb[par][:nt, :], ALU.mult, ALU.mult,
                      ["k32_" + p, "flg", "Eb" + p], ["keb" + p])
            if CUT <= 7:
                return
            pb, pk = C.bank()
            for h in range(GH):
                C.mm(pb[:, 2 * h:2 * h + 2], lgn[:nt, h * 128:(h + 1) * 128], crhs[:nt, 0:2], True, True,
                     [klg, "crhs"], [pk], inc=(h == GH - 1))
            pdv = pb[:, 0:8].rearrange("p (h t) -> p h t", t=2)[:, :, 0]
            if not bwd_only:
                C.act(dec[par][:, 0, :], pdv, AF.Exp, [pk, "nflg"], ["dec" + p], scale=nflg[:, sl, 0:1])
            if not fwd_only:
                C.act(dec[par][:, 1, :], pdv, AF.Exp, [pk, "nflg"], ["dec" + p], scale=nflg[:, sl, 1:2])
            if snap_idx is not None:
                sp_ = tile_no[0] % 2
                C.copy(snapb[sp_][:], Sb[:], ["Sb"], ["snapb%d" % sp_], eng="act")
                C.dma(A["snap%d" % s][snap_idx].rearrange("p (h v) -> p h v", v=GDV), snapb[sp_][:], r=["snapb%d" % sp_],
                      q=STQ)
            for (skip, ke, kek, S, Sk, di) in [(bwd_only, kef, "kef", Sf, "Sf", 0), (fwd_only, keb, "keb", Sb, "Sb", 1)]:
                if skip:
                    continue
                for hp in range(2):
                    pb, pk = C.bank()
                    for hh in range(2):
                        h = 2 * hp + hh
                        C.mm(pb[:, hh * 256:(hh + 1) * 256], ke[par][:nt, h * 128:(h + 1) * 128],
                             gvb[par][:nt, h * 256:(h + 1) * 256], True, True, [kek + p, "gvb" + p], [pk], inc=(hh == 1))
                    for hh in range(2):
                        h = 2 * hp + hh
                        C.stt(S[:, h, :], S[:, h, :], dec[par][:, di, h:h + 1], pb[:, hh * 256:(hh + 1) * 256], ALU.mult,
                              ALU.add, [Sk, "dec" + p, pk], [Sk])

        for s in range(NS):
            T = TQ[s]
            ntl = T // 128
            C.memset(Sf[:], 0.0, ["Sf"])
            C.memset(Sb[:], 0.0, ["Sb"])
            ctx_tile(s, A["meta"], A["rcm"], A["rsm"], NM, 4, 4 * ntl, None)
            for sl in range(4):
                for t in range(min(ntl, TLIM)):
                    kb = (t if sl == 3 else ntl * (sl + 1) + t)
                    ctx_tile(s, A["xc%d" % s][sl, t * 128:(t + 1) * 128, :], A["rcc%d" % s][sl, t * 128:(t + 1) * 128, :],
                             A["rsc%d" % s][sl, t * 128:(t + 1) * 128, :], 128, sl, kb,
                             (ntl - 1 - t) if sl == 3 else None)
            C.copy(snapb[0][:], Sf[:], ["Sf"], ["snapb0"], eng="act")
            C.dma(A["sf%d" % s].rearrange("p (h v) -> p h v", v=GDV), snapb[0][:], r=["snapb0"], q=STQ)
        C.P.build()

    def phase2(part):
      with contextlib.ExitStack() as st:
            C = Ctx(nc, st)
            C.nrot = 6
            C.load_consts(A)
            stg = [C.sb("stg%d" % i, [128, 2048], F32) for i in range(2)]
            gains = C.sb("gains", [128, 16], F32)
            C.dma(gains[:, 0:8], A["attn_norm"], w=["gains"])
            C.dma(gains[:, 8:14], A["q_a_norm"], w=["gains"])
            gon = C.sb("gon", [128, 2], F32)
            C.dma(gon[:], A["gla_o_norm"], w=["gon"])
            gon8 = C.sb("gon8", [128, 8], F32)
            C.copy(gon8[:].rearrange("p (a b) -> p a b", b=2), gon[:].unsqueeze(1).to_broadcast([128, 4, 2]), ["gon"], ["gon8"])
            if part == "q":
                wmain = C.sb("wmain", [128, 8, 768], BF16)
                wuq = C.sb("wuq", [128, 6, H * QK], BF16)
                C.load_weight("wmain", wmain, A["w_main"][:, 0:768], 8, 768, gain=gains[:, 0:8], stg=stg)
                C.load_weight("wuq", wuq, A["w_uq"], 6, H * QK, gain=gains[:, 8:14], stg=stg)
            else:
                wmain = C.sb("wmain", [128, 8, NMAIN - 768], BF16)
                wogla = C.sb("wogla", [128, 8, D], BF16)
                C.load_weight("wmain", wmain, A["w_main"][:, 768:NMAIN], 8, NMAIN - 768, gain=gains[:, 0:8], stg=stg)
                C.load_weight("wogla", wogla, A["w_o_gla"], 8, D, gain=gon8[:, 0:8], stg=stg)
            wa2f = C.sb("wa2f", [17, 2, 512], F32)
            wa2 = C.sb("wa2", [17, 2, 512], BF16)
            C.dma(wa2f[:], A["wa2_m"].rearrange("s k n -> k s n"), w=["wa2f"])
            C.copy(wa2[:], wa2f[:], ["wa2f"], ["wa2"])
            gq = C.sb("gq", [128, QK], F32)
            C.dma(gq[:], A["q_norm"].partition_broadcast(128), w=["gq"])
            C.ts(gq[:], gq[:], float(QK) ** -0.5, ALU.mult, ["gq"], ["gq"])
            cmat = C.sb("cmat", [128, 5, 128], F32)
            crhs = C.sb("crhs", [128, 4], F32)
            C.dma(cmat[:], A["cmat"], w=["cmat"])
            C.dma(crhs[:], A["crhs"], w=["crhs"])
            C.junk = C.sb("junk", [128, 2048], F32)
            C.ssx = [C.sb("ssx%d" % i, [128, 4], F32) for i in range(2)]
            C.hb = [C.sb("hb%d" % i, [128, D], BF16) for i in range(2)]
            C.ab = [C.sb("ab%d" % i, [128, 16], BF16) for i in range(2)]
            C.aT = [C.sb("aT%d" % i, [32, 128], BF16) for i in range(2)]
            C.lgn = [C.sb("lgn%d" % i, [128, 512], F32) for i in range(2)]
            for i in range(2):
                C.memset(C.aT[i][:], 1.0, ["aT%d" % i])
            xt = C.sb("xt", [128, D], F32)
            hT = C.sb("hT", [128, 8, 128], BF16)
            rc = C.sb("rc", [128, 64], F32)
            rs = C.sb("rs", [128, 64], F32)
            if part == "q":
                cqT = C.sb("cqT", [128, 6, 128], BF16)
                qfull = C.sb("qfull", [128, H, QK], F32)
                qn32 = C.sb("qn32", [128, H, QK], F32)
                qr32 = C.sb("qr32", [128, H, 64], F32)
                t1 = C.sb("t1", [128, H, 64], F32)
                t2 = C.sb("t2", [128, H, 64], F32)
                qb = C.sb("qb", [128, H, QK], BF16)
                ssq = C.sb("ssq", [128, 2 * H + 2], F32)
                qtn = C.sb("qtn", [128, H, 128], BF16)
                qtr = C.sb("qtr", [64, H, 128], BF16)
            if part == "gla":
                q32 = C.sb("q32", [128, 512], F32)
                k32 = C.sb("k32", [128, 512], F32)
                gvb = C.sb("gvb", [128, 1024], BF16)
                Eq = C.sb("Eq", [128, 512], F32)
                Ek = C.sb("Ek", [128, 512], F32)
                Eb = C.sb("Eb", [128, 512], F32)
                qd = C.sb("qd", [128, 512], BF16)
                kd = C.sb("kd", [128, 512], BF16)
                ke = C.sb("ke", [128, 512], BF16)
                qdT = [C.sb("qdT%d" % i, [128, GH, 128], BF16) for i in range(2)]
                kdT = [C.sb("kdT%d" % i, [128, GH, 128], BF16) for i in range(2)]
                atb = [C.sb("atb%d" % i, [128, GH, 128], BF16) for i in range(2)]
                dec = C.sb("dec", [128, 2, GH, 2], F32)
                Sf = C.sb("Sf", [128, GH, GDV], F32)
                Sbl = C.sb("Sbl", [128, GH, GDV], F32)
                Sp = [C.sb("Sp%d" % i, [128, GH, GDV], BF16) for i in range(2)]
                o32 = C.sb("o32", [128, GH, GDV], F32)
                sso = C.sb("sso", [128, 2 * GH], F32)
                sg = C.sb("sg", [128, D], F32)
                ogb = C.sb("ogb", [128, D], BF16)
                ogT = C.sb("ogT", [128, 8, 128], BF16)
                siga = C.sb("siga", [128, D], F32)
                sigb = C.sb("sigb", [128, D], F32)
                ybg = C.sb("ybg", [128, D], F32)
            O_CQ = 0
            O_GQ, O_GK, O_GV, O_GG, O_AF, O_GA, O_GB = 0, 512, 1024, 2048, 3072, 3104, 4128

            def zmm(c0, n, nt=128):
                pb, pk = C.bank()
                for c in range(8):
                    C.mm(pb[:nt, 0:n], hT[:, c, 0:nt], wmain[:, c, c0:c0 + n], c == 0, c == 7, ["hT", "wmain"], [pk],
                         inc=(c == 7))
                return pb, pk

            for s in range(NS):
                T = TQ[s]
                ntl = T // 128
                if part == "gla":
                    C.dma(Sf[:], A["sf%d" % s].rearrange("p (h v) -> p h v", v=GDV), w=["Sf"])
                for t in range(ntl):
                    nt = 128
                    tok = slice(t * 128, (t + 1) * 128)
                    C.dma(xt[:], A["xo%d" % s][tok, :], w=["xt"])
                    C.dma(rc[:], A["rco%d" % s][tok, :], w=["rc"])
                    C.dma(rs[:], A["rso%d" % s][tok, :], w=["rs"])
                    if part == "gla":
                        C.dma(Sbl[:], A["snap%d" % s][t].rearrange("p (h v) -> p h v", v=GDV), w=["Sbl"])
                    C.xnorm_T((xt[:], "xt"), 128, 0, hT[:], "hT")
                    if part == "q":
                        pq0, kq0 = zmm(O_CQ, 512)
                        pq1, kq1 = zmm(O_CQ + 512, 256)
                        C.act(C.junk[:, 0:512], pq0[:, :], AF.Square, [kq0], ["junk", "ssq"], accum=ssq[:, 16:17])
                        C.act(C.junk[:, 512:768], pq1[:, 0:256], AF.Square, [kq1], ["junk", "ssq"], accum=ssq[:, 17:18])
                        C.tt(ssq[:, 16:17], ssq[:, 16:17], ssq[:, 17:18], ALU.add, ["ssq"], ["ssq"])
                        C.rstd(ssq[:, 17:18], ssq[:, 16:17], 1.0 / QL, "ssq", "ssq", C.epsb[:, 0:1])
                        cqb = C.hb[1]
                        C.act(cqb[:, 0:512], pq0[:, :], AF.Copy, [kq0, "ssq"], ["cqb"], scale=ssq[:, 17:18])
                        C.act(cqb[:, 512:768], pq1[:, 0:256], AF.Copy, [kq1, "ssq"], ["cqb"], scale=ssq[:, 17:18])
                        pb, pk = C.bank()
                        pvw = pb[:].bitcast(BF16)
                        for c in range(6):
                            C.tr(pvw[:, c * 128:(c + 1) * 128], cqb[:, c * 128:(c + 1) * 128], C.ident[:], ["cqb", "ident"], [pk],
                                 inc=(c == 5))
                        C.copy(cqT[:].rearrange("p c t -> p (c t)"), pvw[:, 0:768], [pk], ["cqT"])
                        for g in range(3):
                            pb, pk = C.bank()
                            for c in range(6):
                                C.mm(pb[:, :], cqT[:, c, :], wuq[:, c, g * 512:(g + 1) * 512], c == 0, c == 5, ["cqT", "wuq"], [pk],
                                     inc=(c == 5))
                            C.copy(qfull[:].rearrange("p h x -> p (h x)")[:, g * 512:(g + 1) * 512], pb[:, :], [pk], ["qfull"],
                                   eng="act")
                        C.act(qn32[:], qfull[:], AF.Square, ["qfull"], ["qn32"])
                        C.red(ssq[:, 0:H], qn32[:], ["qn32"], ["ssq"])
                        C.rstd(ssq[:, H:2 * H], ssq[:, 0:H], 1.0 / QK, "ssq", "ssq", C.epsb[:, 0:1])
                        C.tt(qn32[:], qfull[:], ssq[:, H:2 * H].unsqueeze(2).to_broadcast([128, H, QK]), ALU.mult, ["qfull", "ssq"],
                             ["qn32"])
                        C.tt(qb[:, :, 0:128], qn32[:, :, 0:128], gq[:, 0:128].unsqueeze(1).to_broadcast([128, H, 128]), ALU.mult,
                             ["qn32", "gq"], ["qb"])
                        C.tt(qr32[:], qn32[:, :, 128:192], gq[:, 128:192].unsqueeze(1).to_broadcast([128, H, 64]), ALU.mult,
                             ["qn32", "gq"], ["qr32"])
                        C.tt(t1[:], qr32[:], rc[:].unsqueeze(1).to_broadcast([128, H, 64]), ALU.mult, ["qr32", "rc"], ["t1"])
                        C.tt(t2[:, :, 0:32], qr32[:, :, 32:64], rs[:, 0:32].unsqueeze(1).to_broadcast([128, H, 32]), ALU.mult,
                             ["qr32", "rs"], ["t2"])
                        C.tt(t2[:, :, 32:64], qr32[:, :, 0:32], rs[:, 32:64].unsqueeze(1).to_broadcast([128, H, 32]), ALU.mult,
                             ["qr32", "rs"], ["t2"])
                        C.tt(qb[:, :, 128:192], t1[:], t2[:], ALU.add, ["t1", "t2"], ["qb"])
                        pb, pk = C.bank()
                        pvn = pb[:].bitcast(BF16)
                        pb2, pk2 = C.bank()
                        pvr = pb2[:].bitcast(BF16)
                        for h in range(H):
                            C.tr(pvn[:, h * 128:(h + 1) * 128], qb[:, h, 0:128], C.ident[:], ["qb", "ident"], [pk], inc=False)
                            C.tr(pvr[0:64, h * 128:(h + 1) * 128], qb[:, h, 128:192], C.ident[:], ["qb", "ident"], [pk2],
                                 inc=(h == H - 1))
                        C.copy(qtn[:].rearrange("p h t -> p (h t)"), pvn[:, :], [pk], ["qtn"], eng="act")
                        C.copy(qtr[:].rearrange("p h t -> p (h t)"), pvr[0:64, :], [pk2], ["qtr"], eng="dve")
                        C.dma(A["QTn%d" % s][:, :, tok].rearrange("h d t -> d h t"), qtn[:], r=["qtn"], q=STQ)
                        C.dma(A["QTr%d" % s][:, :, tok].rearrange("h d t -> d h t"), qtr[:], r=["qtr"], q=STQ)
                    if part == "gla":
                        pgq, kgq = zmm(O_GQ, 512)
                        pgk, kgk = zmm(O_GK, 512)
                        pv0, kv0 = zmm(O_GV, 512)
                        pv1, kv1 = zmm(O_GV + 512, 512)
                        pa, ka = zmm(O_AF, 32)
                        C.act(q32[:], pgq[:, :], AF.Copy, [kgq], ["q32"], scale=float(GDK) ** -0.5)
                        C.copy(k32[:], pgk[:, :], [kgk], ["k32"], eng="act")
                        C.copy(gvb[:, 0:512], pv0[:, :], [kv0], ["gvb"], eng="dve")
                        C.copy(gvb[:, 512:1024], pv1[:, :], [kv1], ["gvb"], eng="dve")
                        lgns = []
                        for di in range(2):
                            C.copy(C.ab[di][:, :], pa[:, 16 * di:16 * di + 16], [ka], ["ab%d" % di])
                        for di in range(2):
                            lgns.append(gate_path(C, 128, wa2[0:17, di, :], "wa2", di))
                        ob = [(C.banks[6], "pb6"), (C.banks[7], "pb7")]
                        for di in range(2):
                            lgn, klg = lgns[di]
                            pb, pk = C.bank()
                            C.mm(pb[:, :], cmat[:, 1 + 2 * di, :], lgn[:, :], True, True, ["cmat", klg], [pk])
                            C.act(Eq[:], pb[:, :], AF.Exp, [pk], ["Eq"], scale=-1.0 / 16.0)
                            C.act(Ek[:], pb[:, :], AF.Exp, [pk], ["Ek"], scale=1.0 / 16.0)
                            C.tt(qd[:], q32[:], Eq[:], ALU.mult, ["q32", "Eq"], ["qd"])
                            C.tt(kd[:], k32[:], Ek[:], ALU.mult, ["k32", "Ek"], ["kd"])
                            pb, pk = C.bank()
                            pvw = pb[:].bitcast(BF16)
                            for h in range(GH):
                                C.tr(pvw[:, h * 128:(h + 1) * 128], qd[:, h * 128:(h + 1) * 128], C.ident[:], ["qd", "ident"], [pk],
                                     inc=False)
                                C.tr(pvw[:, 512 + h * 128:512 + (h + 1) * 128], kd[:, h * 128:(h + 1) * 128], C.ident[:],
                                     ["kd", "ident"], [pk], inc=(h == GH - 1))
                            C.copy(qdT[di][:].rearrange("p h t -> p (h t)"), pvw[:, 0:512], [pk], ["qdT%d" % di], eng="act")
                            C.copy(kdT[di][:].rearrange("p h t -> p (h t)"), pvw[:, 512:1024], [pk], ["kdT%d" % di], eng="dve")
                            pb, pk = C.bank()
                            for h in range(GH):
                                C.mm(pb[:, h * 128:(h + 1) * 128], kdT[di][:, h, :], qdT[di][:, h, :], True, True,
                                     ["kdT%d" % di, "qdT%d" % di], [pk], inc=(h == GH - 1))
                            C.tt(atb[di][:], pb[:, :].rearrange("p (h t) -> p h t", t=128),
                                 cmat[:, 2 + 2 * di, :].unsqueeze(1).to_broadcast([128, GH, 128]), ALU.mult, [pk, "cmat"],
                                 ["atb%d" % di])
                            pb, pk = C.bank()
                            for h in range(GH):
                                C.mm(pb[:, 2 * h:2 * h + 2], lgn[:, h * 128:(h + 1) * 128], crhs[:, 2 * di:2 * di + 2], True, True,
                                     [klg, "crhs"], [pk], inc=(h == GH - 1))
                            C.act(dec[:, di].rearrange("p h t -> p (h t)"), pb[:, 0:8], AF.Exp, [pk], ["dec"], scale=-1.0 / 16.0)
                            S = Sf if di == 0 else Sbl
                            Sk = "Sf" if di == 0 else "Sbl"
                            for h in range(GH):
                                C.ts(Sp[di][:, h, :], S[:, h, :], dec[:, di, h, 1:2], ALU.mult, [Sk, "dec"], ["Sp%d" % di])
                        for di in range(2):
                            for h in range(GH):
                                pb, pk = ob[h // 2]
                                osl = pb[:, (h % 2) * 256:(h % 2 + 1) * 256]
                                C.mm(osl, atb[di][:, h, :], gvb[:, h * 256:(h + 1) * 256], (di == 0 and h % 2 == 0), False,
                                     ["atb%d" % di, "gvb"], [pk], inc=False, sgc=True)
                                C.mm(osl, qdT[di][:, h, :], Sp[di][:, h, :], False, di == 1, ["qdT%d" % di, "Sp%d" % di], [pk],
                                     inc=(di == 1 and h % 2 == 1), sgc=True)
                        lgn, klg = lgns[0]
                        pb, pk = C.bank()
                        C.mm(pb[:, :], cmat[:, 0, :], lgn[:, :], True, True, ["cmat", klg], [pk])
                        C.act(Eb[:], pb[:, :], AF.Exp, [pk], ["Eb"], scale=-1.0 / 16.0)
                        C.tt(ke[:], k32[:], Eb[:], ALU.mult, ["k32", "Eb"], ["ke"])
                        for hp in range(2):
                            pb, pk = C.bank()
                            for hh in range(2):
                                h = 2 * hp + hh
                                C.mm(pb[:, hh * 256:(hh + 1) * 256], ke[:, h * 128:(h + 1) * 128], gvb[:, h * 256:(h + 1) * 256],
                                     True, True, ["ke", "gvb"], [pk], inc=(hh == 1))
                            for hh in range(2):
                                h = 2 * hp + hh
                                C.stt(Sf[:, h, :], Sf[:, h, :], dec[:, 0, h, 0:1], pb[:, hh * 256:(hh + 1) * 256], ALU.mult, ALU.add,
                                      ["Sf", "dec", pk], ["Sf"])
                        for hp in range(2):
                            C.copy(o32[:, 2 * hp:2 * hp + 2, :].rearrange("p h v -> p (h v)"), ob[hp][0][:, :], [ob[hp][1]], ["o32"],
                                   eng="act")
                        C.act(C.junk[:, 0:1024], o32[:].rearrange("p h v -> p (h v)"), AF.Square, ["o32"], ["junk"])
                        C.red(sso[:, 0:GH], C.junk[:, 0:1024].rearrange("p (h v) -> p h v", v=GDV), ["junk"], ["sso"])
                        C.rstd(sso[:, GH:2 * GH], sso[:, 0:GH], 1.0 / GDV, "sso", "sso", C.epsb[:, 0:1])
                        pg0, kg0 = zmm(O_GG, 512)
                        pg1, kg1 = zmm(O_GG + 512, 512)
                        C.act(sg[:, 0:512], pg0[:, :], AF.Silu, [kg0], ["sg"])
                        C.act(sg[:, 512:1024], pg1[:, :], AF.Silu, [kg1], ["sg"])
                        pa0, ka0 = zmm(O_GA, 512)
                        pa1, ka1 = zmm(O_GA + 512, 512)
                        C.act(siga[:, 0:512], pa0[:, :], AF.Sigmoid, [ka0], ["siga"])
                        C.act(siga[:, 512:1024], pa1[:, :], AF.Sigmoid, [ka1], ["siga"])
                        pb0, kb0 = zmm(O_GB, 512)
                        pb1, kb1 = zmm(O_GB + 512, 512)
                        C.act(sigb[:, 0:512], pb0[:, :], AF.Sigmoid, [kb0], ["sigb"])
                        C.act(sigb[:, 512:1024], pb1[:, :], AF.Sigmoid, [kb1], ["sigb"])
                        C.dma(A["siga%d" % s][tok, :], siga[:], r=["siga"], q=STQ)
                        C.tt(o32[:], o32[:], sso[:, GH:2 * GH].unsqueeze(2).to_broadcast([128, GH, GDV]), ALU.mult, ["o32", "sso"],
                             ["o32"])
                        C.tt(ogb[:], o32[:].rearrange("p h v -> p (h v)"), sg[:], ALU.mult, ["o32", "sg"], ["ogb"])
                        pb, pk = C.bank()
                        pvw = pb[:].bitcast(BF16)
                        for c in range(8):
                            C.tr(pvw[:, c * 128:(c + 1) * 128], ogb[:, c * 128:(c + 1) * 128], C.ident[:], ["ogb", "ident"], [pk],
                                 inc=(c == 7))
                        C.copy(ogT[:].rearrange("p c t -> p (c t)"), pvw[:, :], [pk], ["ogT"])
                        for g in range(2):
                            pb, pk = C.bank()
                            for c in range(8):
                                C.mm(pb[:, :], ogT[:, c, :], wogla[:, c, g * 512:(g + 1) * 512], c == 0, c == 7, ["ogT", "wogla"],
                                     [pk], inc=(c == 7))
                            C.tt(ybg[:, g * 512:(g + 1) * 512], pb[:, :], sigb[:, g * 512:(g + 1) * 512], ALU.mult, [pk, "sigb"],
                                 ["ybg"])
                        C.dma(A["ybg%d" % s][tok, :], ybg[:], r=["ybg"], q=STQ)
            C.P.build()

    if NPH >= 2:
        phase2("q")
    if NPH >= 3:
        phase2("gla")

    with contextlib.ExitStack() as st:
      if NPH >= 4:
        C = Ctx(nc, st)
        onesb = C.sb("onesb", [128, 128], BF16)
        C.memset(onesb[:], 1.0, ["onesb"])
        KB = 16
        LA = 2
        qn_t = [C.sb("qn_t%d" % i, [128, 512], BF16) for i in range(2)]
        qr_t = [C.sb("qr_t%d" % i, [64, 512], BF16) for i in range(2)]
        kn_t = [C.sb("kn_t%d" % i, [128, KB * 128], BF16) for i in range(3)]
        kr_t = [C.sb("kr_t%d" % i, [64, KB * 128], BF16) for i in range(3)]
        v_t = [C.sb("v_t%d" % i, [128, KB, 128], BF16) for i in range(3)]
        pbuf = [C.sb("pbuf%d" % i, [128, 512], BF16) for i in range(4)]
        rden = C.sb("rden", [128, 512], F32)
        otb = [C.sb("otb%d" % i, [128, 512], BF16) for i in range(2)]
        sbanks = [(C.banks[i], "pb%d" % i) for i in range(4)]
        groups = []
        chunks = []
        blocks = []
        for s in range(NS):
            T = TQ[s]
            nfull = 4 * (T // 128)
            for q0 in range(0, T, 512):
                nq = min(512, T - q0)
                for h in range(H):
                    gi = len(groups)
                    groups.append((s, q0, nq, h))
                    nb_tot = nfull + 1
                    for cb in range(0, nb_tot, KB):
                        nb = min(KB, nb_tot - cb)
                        ck = len(chunks)
                        chunks.append((gi, cb, nb))
                        for bi in range(nb):
                            b = cb + bi
                            blocks.append((gi, ck, bi, (128 if b < nfull else NM), b == 0, b == nb_tot - 1))

        def load_q(gi):
            s, q0, nq, h = groups[gi]
            qi = gi % 2
            C.dma(qn_t[qi][:, 0:nq], A["QTn%d" % s][h, :, q0:q0 + nq], w=["qn_t%d" % qi])
            C.dma(qr_t[qi][:, 0:nq], A["QTr%d" % s][h, :, q0:q0 + nq], w=["qr_t%d" % qi])

        def load_chunk(ck):
            gi, cb, nb = chunks[ck]
            s, q0, nq, h = groups[gi]
            ci = ck % 3
            C.dma(kn_t[ci][:, 0:nb * 128], A["KTn%d" % s][h, :, cb * 128:(cb + nb) * 128], w=["kn_t%d" % ci])
            C.dma(kr_t[ci][:, 0:nb * 128], A["KTr%d" % s][h, :, cb * 128:(cb + nb) * 128], w=["kr_t%d" % ci])
            C.dma(v_t[ci][:, 0:nb, :], A["Vs%d" % s][h, :, cb:cb + nb, :], w=["v_t%d" % ci])

        def emit_S(i):
            gi, ck, bi, nk, first, last = blocks[i]
            s, q0, nq, h = groups[gi]
            qi, ci = gi % 2, ck % 3
            sb_, skey = sbanks[i % 4]
            C.mm(sb_[:nk, 0:nq], kn_t[ci][:, bi * 128:bi * 128 + nk], qn_t[qi][:, 0:nq], True, False,
                 ["kn_t%d" % ci, "qn_t%d" % qi], [skey], inc=False)
            C.mm(sb_[:nk, 0:nq], kr_t[ci][:, bi * 128:bi * 128 + nk], qr_t[qi][:, 0:nq], False, True,
                 ["kr_t%d" % ci, "qr_t%d" % qi], [skey])
            if last and gi + 2 < len(groups):
                load_q(gi + 2)

        def emit_PV(j):
            gi, ck, bi, nk, first, last = blocks[j]
            s, q0, nq, h = groups[gi]
            qi, ci, pi = gi % 2, ck % 3, j % 4
            sb_, skey = sbanks[j % 4]
            ob, okey = C.banks[4 + qi], "pb%d" % (4 + qi)
            db, dkey = C.banks[6 + qi], "pb%d" % (6 + qi)
            C.act(pbuf[pi][:nk, 0:nq], sb_[:nk, 0:nq], AF.Exp, [skey], ["pbuf%d" % pi])
            C.mm(ob[:, 0:nq], v_t[ci][:nk, bi, :], pbuf[pi][:nk, 0:nq], first, last, ["v_t%d" % ci, "pbuf%d" % pi], [okey],
                 inc=False)
            C.mm(db[:, 0:nq], onesb[:nk, :], pbuf[pi][:nk, 0:nq], first, last, ["onesb", "pbuf%d" % pi], [dkey])
            if bi == chunks[ck][2] - 1 and ck + 3 < len(chunks):
                load_chunk(ck + 3)
            if last:
                C.P.op("dve", lambda e, db=db, nq=nq: e.reciprocal(out=rden[:, 0:nq], in_=db[:, 0:nq]), [dkey], ["rden"])
                C.tt(otb[qi][:, 0:nq], ob[:, 0:nq], rden[:, 0:nq], ALU.mult, [okey, "rden"], ["otb%d" % qi])
                C.dma(A["OT%d" % s][h, :, q0:q0 + nq], otb[qi][:, 0:nq], r=["otb%d" % qi], q=STQ)

        NB = len(blocks)
        for g0 in range(min(2, len(groups))):
            load_q(g0)
        for c0 in range(min(3, len(chunks))):
            load_chunk(c0)
        for i in range(NB + LA):
            if i < NB:
                emit_S(i)
            if i - LA >= 0:
                emit_PV(i - LA)
        C.P.build()

    if NPH < 5:
        return nc
    with contextlib.ExitStack() as st:
        C = Ctx(nc, st)
        C.load_consts(A)
        stg = [C.sb("stg%d" % i, [128, 2048], F32) for i in range(2)]
        womla = C.sb("womla", [128, 8, D], BF16)
        wout = C.sb("wout", [128, 8, D], BF16)
        C.load_weight("womla", womla, A["w_o_mla"], 8, D, stg=stg)
        C.load_weight("wout", wout, A["w_out"], 8, D, stg=stg)
        xt = [C.sb("xt%d" % i, [128, D], F32) for i in range(2)]
        sga = [C.sb("sga%d" % i, [128, D], F32) for i in range(2)]
        ybg = [C.sb("ybg%d" % i, [128, D], F32) for i in range(2)]
        ot = [C.sb("ot%d" % i, [128, H, 128], BF16) for i in range(2)]
        mixb = [C.sb("mixb%d" % i, [128, D], BF16) for i in range(2)]
        mixT = [C.sb("mixT%d" % i, [128, 8, 128], BF16) for i in range(2)]
        n = 0
        for s in range(NS):
            T = TQ[s]
            for t in range(T // 128):
                i = n % 2
                n += 1
                p = "%d" % i
                tok = slice(t * 128, (t + 1) * 128)
                C.dma(xt[i][:], A["xo%d" % s][tok, :], w=["xt" + p])
                C.dma(sga[i][:], A["siga%d" % s][tok, :], w=["sga" + p])
                C.dma(ybg[i][:], A["ybg%d" % s][tok, :], w=["ybg" + p])
                C.dma(ot[i][:], A["OT%d" % s][:, :, tok].rearrange("h v t -> v h t"), w=["ot" + p])
                for g in range(2):
                    pb, pk = C.bank()
                    for h in range(H):
                        C.mm(pb[:, :], ot[i][:, h, :], womla[:, h, g * 512:(g + 1) * 512], h == 0, h == H - 1,
                             ["ot" + p, "womla"], [pk], inc=(h == H - 1))
                    C.tt(sga[i][:, g * 512:(g + 1) * 512], pb[:, :], sga[i][:, g * 512:(g + 1) * 512], ALU.mult,
                         [pk, "sga" + p], ["sga" + p])
                C.tt(mixb[i][:], sga[i][:], ybg[i][:], ALU.add, ["sga" + p, "ybg" + p], ["mixb" + p])
                pb, pk = C.bank()
                pvw = pb[:].bitcast(BF16)
                for c in range(8):
                    C.tr(pvw[:, c * 128:(c + 1) * 128], mixb[i][:, c * 128:(c + 1) * 128], C.ident[:], ["mixb" + p, "ident"],
                         [pk], inc=(c == 7))
                C.copy(mixT[i][:].rearrange("p c t -> p (c t)"), pvw[:, :], [pk], ["mixT" + p], eng="act")
                for g in range(2):
                    pb, pk = C.bank()
                    for c in range(8):
                        C.mm(pb[:, :], mixT[i][:, c, :], wout[:, c, g * 512:(g + 1) * 512], c == 0, c == 7,
                             ["mixT" + p, "wout"], [pk], inc=(c == 7))
                    C.tt(xt[i][:, g * 512:(g + 1) * 512], pb[:, :], xt[i][:, g * 512:(g + 1) * 512], ALU.add, [pk, "xt" + p],
                         ["xt" + p])
                C.dma(A["x1_%d" % s][tok, :], xt[i][:], r=["xt" + p], q=STQ)
        C.P.build()

    if NPH < 6:
        return nc
    with contextlib.ExitStack() as st:
        C = Ctx(nc, st)
        C.load_consts(A)
        stg = [C.sb("stg%d" % i, [128, 2048], F32) for i in range(2)]
        gains = C.sb("gains", [128, 8], F32)
        C.dma(gains[:, 0:8], A["ffn_norm"], w=["gains"])
        wg = C.sb("wg", [128, 8, DFF], BF16)
        wu = C.sb("wu", [128, 8, DFF], BF16)
        wd = C.sb("wd", [128, 22, D], BF16)
        C.load_weight("wg", wg, A["w_ffn_gate"], 8, DFF, gain=gains[:, 0:8], stg=stg)
        C.load_weight("wu", wu, A["w_ffn_up"], 8, DFF, gain=gains[:, 0:8], stg=stg)
        C.load_weight("wd", wd, A["w_ffn_down"], 22, D, stg=stg)
        C.junk = C.sb("junk", [128, D], F32)
        C.ssx = [C.sb("ssx%d" % i, [128, 4], F32) for i in range(2)]
        C.hb = [C.sb("hb%d" % i, [128, D], BF16) for i in range(2)]
        xt = [C.sb("xt%d" % i, [128, D], F32) for i in range(2)]
        h2T = [C.sb("h2T%d" % i, [128, 8, 128], BF16) for i in range(2)]
        sgl = [C.sb("sgl%d" % i, [128, 512], F32) for i in range(2)]
        actb = C.sb("actb", [128, DFF], BF16)
        actT = C.sb("actT", [128, 22, 128], BF16)
        yo = [C.sb("yo%d" % i, [128, D], F32) for i in range(2)]
        n = 0
        for s in range(NS):
            T = TQ[s]
            for t in range(T // 128):
                i = n % 2
                n += 1
                p = "%d" % i
                tok = slice(t * 128, (t + 1) * 128)
                C.dma(xt[i][:], A["x1_%d" % s][tok, :], w=["xt" + p])
                C.xnorm_T((xt[i][:], "xt" + p), 128, i, h2T[i][:], "h2T" + p)
                cg = 0
                for c0 in range(0, DFF, 512):
                    nn = min(512, DFF - c0)
                    pg, kg = C.bank()
                    pu, ku = C.bank()
                    for c in range(8):
                        C.mm(pg[:, 0:nn], h2T[i][:, c, :], wg[:, c, c0:c0 + nn], c == 0, c == 7, ["h2T" + p, "wg"], [kg],
                             inc=False)
                    for c in range(8):
                        C.mm(pu[:, 0:nn], h2T[i][:, c, :], wu[:, c, c0:c0 + nn], c == 0, c == 7, ["h2T" + p, "wu"], [ku],
                             inc=(c == 7))
                    sp_ = cg % 2
                    cg += 1
                    C.act(sgl[sp_][:, 0:nn], pg[:, 0:nn], AF.Silu, [kg], ["sgl%d" % sp_])
                    C.tt(actb[:, c0:c0 + nn], pu[:, 0:nn], sgl[sp_][:, 0:nn], ALU.mult, [ku, "sgl%d" % sp_], ["actb"])
                for c4 in range(0, 22, 8):
                    ncb = min(8, 22 - c4)
                    pb, pk = C.bank()
                    pvw = pb[:].bitcast(BF16)
                    for c in range(ncb):
                        C.tr(pvw[:, c * 128:(c + 1) * 128], actb[:, (c4 + c) * 128:(c4 + c + 1) * 128], C.ident[:],
                             ["actb", "ident"], [pk], inc=(c == ncb - 1))
                    C.copy(actT[:, c4:c4 + ncb, :].rearrange("p c t -> p (c t)"), pvw[:, 0:ncb * 128], [pk], ["actT"],
                           eng=("act" if (c4 // 8) % 2 == 0 else "dve"))
                for g in range(2):
                    pb, pk = C.bank()
                    for c in range(22):
                        C.mm(pb[:, :], actT[:, c, :], wd[:, c, g * 512:(g + 1) * 512], c == 0, c == 21, ["actT", "wd"], [pk],
                             inc=(c == 21))
                    C.tt(yo[i][:, g * 512:(g + 1) * 512], pb[:, :], xt[i][:, g * 512:(g + 1) * 512], ALU.add, [pk, "xt" + p],
                         ["yo" + p])
                C.dma(A["y%d" % s][tok, :], yo[i][:], r=["yo" + p], q=STQ)
        C.P.build()
    return nc


def _rope_tables(length):
    inv = (1.0 / (np.float32(10000.0) ** (np.arange(0, ROPE, 2, dtype=np.float32) / np.float32(ROPE)))).astype(np.float32)
    ang = (np.arange(length, dtype=np.float32)[:, None] * inv[None, :]).astype(np.float32)
    c = np.cos(ang).astype(np.float32)
    s = np.sin(ang).astype(np.float32)
    return np.concatenate([c, c], 1), np.concatenate([-s, s], 1)


def _consts():
    j = np.arange(128)[:, None]
    i = np.arange(128)[None, :]
    cm = np.zeros((128, 5, 128), np.float32)
    cm[:, 0] = (j > i)
    cm[:, 1] = (j <= i).astype(np.float32) - (j <= 63)
    cm[:, 2] = (j <= i)
    cm[:, 3] = (j >= i).astype(np.float32) - (j >= 64)
    cm[:, 4] = (j >= i)
    cr = np.zeros((128, 4), np.float32)
    cr[:, 0] = 1.0
    cr[:, 1] = (np.arange(128) <= 63)
    cr[:, 2] = 1.0
    cr[:, 3] = (np.arange(128) >= 64)
    return cm, cr


def make_in_maps(inp, TQ, n_groups, seq_arrays):
    f = lambda a: np.ascontiguousarray(np.asarray(a, dtype=np.float32))
    w_in = f(inp["w_in"][0])
    offs = np.cumsum([0, QL, KVL, ROPE, 512, 512, 1024, 1024, 16, 16, 1024, 1024])
    seg = lambda i: w_in[:, offs[i]:offs[i + 1]]
    w_ctx = np.concatenate([seg(1), seg(2), seg(4), seg(5)], 1)
    w_main = np.concatenate([seg(0), seg(3), seg(4), seg(5), seg(6), seg(7), seg(8), seg(9), seg(10)], 1)
    wa = [seg(7), seg(8)]
    wa2 = [np.concatenate([f(inp["w_a2_fwd"][0]), f(inp["b_a2_fwd"])], 0),
           np.concatenate([f(inp["w_a2_bwd"][0]), f(inp["b_a2_bwd"])], 0)]
    cm, cr = _consts()
    shared = {"meta": f(inp["meta_tokens"]), "w_ctx": f(w_ctx), "w_main": f(w_main), "w_uq": f(inp["w_uq"][0]),
              "w_ukv": f(inp["w_ukv"][0]), "w_o_mla": f(inp["w_o_mla"][0]), "w_o_gla": f(inp["w_o_gla"][0]),
              "w_out": f(inp["w_out"][0]), "w_ffn_gate": f(inp["w_ffn_gate"][0]), "w_ffn_up": f(inp["w_ffn_up"][0]),
              "w_ffn_down": f(inp["w_ffn_down"][0]), "attn_norm": f(np.asarray(inp["attn_norm"], np.float32).reshape(-1, 128).T), "q_a_norm": f(np.asarray(inp["q_a_norm"], np.float32).reshape(-1, 128).T),
              "kv_a_norm": f(np.asarray(inp["kv_a_norm"], np.float32).reshape(-1, 128).T), "q_norm": f(inp["q_norm"]), "k_norm": f(inp["k_norm"]),
              "gla_o_norm": f(np.asarray(inp["gla_o_norm"], np.float32).reshape(-1, 128).T), "ffn_norm": f(np.asarray(inp["ffn_norm"], np.float32).reshape(-1, 128).T), "cmat": cm, "crhs": cr,
              "wa2_m": np.stack(wa2, 0)}
    rt = [_rope_tables(NM + 4 * T) for T in TQ]
    shared["rcm"] = f(rt[0][0][:NM])
    shared["rsm"] = f(rt[0][1][:NM])
    maps = []
    for g in range(n_groups):
        for j in range(4):
            m = dict(shared)
            slots = [(qq, 0) for qq in range(j)] + [(qq, 1) for qq in range(3, j, -1)] + [(j, 1)]
            flags = np.zeros((5, 128, 2), np.float32)
            for sl, (qq, d) in enumerate(slots):
                flags[sl, :, d] = 1.0
            flags[4, :, 0] = 1.0
            m["flags"] = flags
            dirs = [d for (_, d) in slots] + [0]
            m["wa_c"] = f(np.stack([wa[d] for d in dirs], 0))
            m["wa2_c"] = f(np.stack([wa2[d] for d in dirs], 0))
            for s, T in enumerate(TQ):
                x = seq_arrays[s][g]
                rcf, rsf = rt[s]
                pos = lambda qq: np.arange(qq * T, (qq + 1) * T)
                m["xo%d" % s] = f(x[pos(j)])
                m["rco%d" % s] = f(rcf[NM + pos(j)])
                m["rso%d" % s] = f(rsf[NM + pos(j)])
                xs, cs, ss_ = [], [], []
                for (qq, d) in slots:
                    idx = pos(qq)[::-1] if d == 1 else pos(qq)
                    xs.append(x[idx])
                    cs.append(rcf[NM + idx])
                    ss_.append(rsf[NM + idx])
                m["xc%d" % s] = f(np.stack(xs, 0))
                m["rcc%d" % s] = f(np.stack(cs, 0))
                m["rsc%d" % s] = f(np.stack(ss_, 0))
            maps.append(m)
    return maps


_CACHE = {}


def run(inp, TQ, seq_arrays, n_groups):
    key = tuple(TQ)
    if key not in _CACHE:
        _CACHE[key] = build(list(TQ))
    nc = _CACHE[key]
    maps = make_in_maps(inp, TQ, n_groups, seq_arrays)
    res = run_bass_kernel_spmd(nc, maps, core_ids=list(range(len(maps))))
    outs = []
    for s, T in enumerate(TQ):
        y = np.zeros((n_groups, 4 * T, D), np.float32)
        for g in range(n_groups):
            for j in range(4):
                y[g, j * T:(j + 1) * T] = res.results[g * 4 + j]["y%d" % s]
        outs.append(y)
    return outs


def kernel(**inputs):
    xp = np.asarray(inputs["x_prompt"], dtype=np.float32)
    xs = np.asarray(inputs["x_sample"], dtype=np.float32)
    TQ = [xs.shape[1] // 4, xp.shape[1] // 4]
    ys, yp = run(inputs, TQ, [xs, xp], 2)
    return (yp, ys)
```

```python
import contextlib
import numpy as np
import concourse.bass as bass
import concourse.mybir as mybir
from concourse.bass_utils import run_bass_kernel_spmd

F32 = mybir.dt.float32
BF16 = mybir.dt.bfloat16
AF = mybir.ActivationFunctionType
ALU = mybir.AluOpType
AX = mybir.AxisListType

D = 1024
NM = 16
H = 8
NOPE = 128
ROPE = 64
QK = 192
VD = 128
QL = 768
KVL = 256
GH = 4
GDK = 128
GDV = 256
DFF = 2816
EPS = 1e-6
NCTX = 1856
NMAIN = 5920
N_DMA_SEMS = 10
import os
STQ = os.environ.get('STQ', 'sp')
CUT = int(os.environ.get('CUT', '99'))
TLIM = int(os.environ.get('TLIM', '100000'))


class Prog:
    ENGS = ("pe", "act", "dve", "pool", "sp")
    EPOCH = 30000

    G = None

    def __init__(self, nc):
        self.nc = nc
        self.streams = {e: [] for e in self.ENGS}
        g = Prog.G
        self.cnt = g["cnt"]
        self.sems = g["sems"]
        self.epoch = g["epoch"]
        self.known = {e: dict(self.cnt) for e in self.ENGS}
        self.last_w = {}
        self.readers = {}
        self.dma_rr = 0

    def _pe(self, eng):
        ep = self.epoch.setdefault(eng, 0)
        name = "%s_%d" % (eng, ep)
        if self.cnt.get(name, 0) >= self.EPOCH:
            self.epoch[eng] = ep + 1
            name = "%s_%d" % (eng, ep + 1)
        return name

    def _deps(self, reads, writes):
        need = {}

        def add(pe, v):
            if need.get(pe, 0) < v:
                need[pe] = v

        for k in reads:
            for pe, v in self.last_w.get(k, {}).items():
                add(pe, v)
        for k in writes:
            for pe, v in self.last_w.get(k, {}).items():
                add(pe, v)
            for pe, v in self.readers.get(k, {}).items():
                add(pe, v)
        return need

    def _filter(self, eng, need):
        waits = []
        kn = self.known[eng]
        for pe, v in need.items():
            if eng == "pe" and pe.startswith("pe_"):
                continue
            if kn.get(pe, 0) >= v:
                continue
            kn[pe] = v
            waits.append((pe, v))
        return waits

    def _record(self, reads, writes, pe, v):
        for k in reads:
            self.readers.setdefault(k, {})[pe] = v
        for k in writes:
            self.last_w.setdefault(k, {})[pe] = v

    def op(self, eng, fn, reads=(), writes=(), inc=True):
        if eng != "pe":
            writes = list(writes) + [k for k in reads if k.startswith("pb")]
        need = self._deps(reads, writes)
        waits = self._filter(eng, need)
        pe = self._pe(eng)
        v = self.cnt.get(pe, 0) + 1
        if inc:
            self.cnt[pe] = v
            self.streams[eng].append((waits, fn, (pe, 1)))
        else:
            assert eng == "pe"
            self.streams[eng].append((waits, fn, None))
        self._record(reads, writes, pe, v)

    def dma(self, fn, reads=(), writes=(), q="sp"):
        s = self.dma_rr
        self.dma_rr = (self.dma_rr + 1) % N_DMA_SEMS
        ep = self.epoch.setdefault("q%d" % s, 0)
        if self.cnt.get("q%d_%d" % (s, ep), 0) >= 1800:
            ep += 1
            self.epoch["q%d" % s] = ep
        pe = "q%d_%d" % (s, ep)
        need = self._deps(reads, writes)
        prev = self.cnt.get(pe, 0)
        if prev:
            need[pe] = max(need.get(pe, 0), prev)
        waits = self._filter(q, need)
        v = prev + 1
        self.cnt[pe] = v
        self.streams[q].append((waits, fn, (pe, 16)))
        self._record(reads, writes, pe, v)

    def build(self):
        nc = self.nc
        for e in self.ENGS:
            waits = self._filter(e, dict(self.cnt))
            if waits:
                self.streams[e].append((waits, None, None))
        with contextlib.ExitStack() as st:
            for n in list(self.cnt.keys()):
                if n not in self.sems:
                    self.sems[n] = Prog.G["st"].enter_context(nc.semaphore("s_" + n))
            block = st.enter_context(nc.Block())

            def run(engh, stream):
                for waits, fn, inc in stream:
                    for pe, v in waits:
                        engh.wait_ge(self.sems[pe], v * (16 if pe.startswith("q") else 1))
                    if fn is None:
                        continue
                    ins = fn(engh)
                    if inc is not None:
                        ins.then_inc(self.sems[inc[0]], inc[1])

            @block.sync
            def _(e):
                run(e, self.streams["sp"])

            @block.tensor
            def _(e):
                run(e, self.streams["pe"])

            @block.scalar
            def _(e):
                run(e, self.streams["act"])

            @block.vector
            def _(e):
                run(e, self.streams["dve"])

            @block.gpsimd
            def _(e):
                run(e, self.streams["pool"])


class Ctx:
    PH = [0]

    def __init__(self, nc, st):
        self.nc = nc
        self.st = st
        Ctx.PH[0] += 1
        self.pfx = "ph%d_" % Ctx.PH[0]
        self.P = Prog(nc)
        self.P.pfx = self.pfx
        self.banks = [st.enter_context(nc.psum_tensor(self.pfx + "pb%d" % i, [128, 512], F32)) for i in range(8)]
        self.bank_rr = 0
        self.nrot = 8
        self.uid = 0

    def sb(self, name, shape, dt):
        return self.st.enter_context(self.nc.sbuf_tensor(self.pfx + name, shape, dt))

    def bank(self):
        i = self.bank_rr
        self.bank_rr = (i + 1) % self.nrot
        return self.banks[i], "pb%d" % i

    def act(self, out, in_, func, r, w, scale=1.0, bias=0.0, accum=None):
        if accum is None:
            self.P.op("act", lambda e: e.activation(out=out, in_=in_, func=func, scale=scale, bias=bias), r, w)
        else:
            self.P.op("act", lambda e: e.activation(out=out, in_=in_, func=func, scale=scale, bias=bias,
                                                    accum_out=accum), r, w)

    def tt(self, out, in0, in1, op, r, w, eng="dve"):
        self.P.op(eng, lambda e: e.tensor_tensor(out=out, in0=in0, in1=in1, op=op), r, w)

    def ts(self, out, in0, s1, op0, r, w, s2=None, op1=None, eng="dve"):
        if op1 is None:
            self.P.op(eng, lambda e: e.tensor_scalar(out=out, in0=in0, scalar1=s1, scalar2=None, op0=op0), r, w)
        else:
            self.P.op(eng, lambda e: e.tensor_scalar(out=out, in0=in0, scalar1=s1, scalar2=s2, op0=op0, op1=op1), r, w)

    def stt(self, out, in0, scalar, in1, op0, op1, r, w):
        self.P.op("dve", lambda e: e.scalar_tensor_tensor(out=out, in0=in0, scalar=scalar, in1=in1, op0=op0, op1=op1), r, w)

    def copy(self, out, in_, r, w, eng="dve"):
        if eng == "act":
            self.P.op("act", lambda e: e.copy(out=out, in_=in_), r, w)
        else:
            self.P.op(eng, lambda e: e.tensor_copy(out=out, in_=in_), r, w)

    def red(self, out, in_, r, w):
        self.P.op("dve", lambda e: e.tensor_reduce(out=out, in_=in_, axis=AX.X, op=ALU.add), r, w)

    def mm(self, out, lhsT, rhs, start, stop, r, w, inc=True, sgc=False):
        self.P.op("pe", lambda e: e.matmul(out, lhsT=lhsT, rhs=rhs, start=start, stop=stop, skip_group_check=sgc), r, w,
                  inc=inc)

    def tr(self, out, in_, ident, r, w, inc=True):
        self.P.op("pe", lambda e: e.transpose(out=out, in_=in_, identity=ident), r, w, inc=inc)

    def dma(self, out, in_, r=(), w=(), q="sp"):
        self.P.dma(lambda e: e.dma_start(out=out, in_=in_), r, w, q=q)

    def memset(self, ap, val, w, eng="pool"):
        self.P.op(eng, lambda e: e.memset(ap, val), (), w)

    def rstd(self, out, ss, inv_n, key_ss, key_out, epsb):
        self.act(out, ss, AF.Ln, [key_ss, "epsb"], [key_out], scale=inv_n, bias=epsb)
        self.act(out, out, AF.Exp, [key_out], [key_out], scale=-0.5)

    def load_consts(self, A):
        nc = self.nc
        self.identf = self.sb("identf", [128, 128], F32)
        self.ident = self.sb("ident", [128, 128], BF16)
        self.epsb = self.sb("epsb", [128, 1], F32)
        self.P.op("pool", lambda e: e.memset(self.identf[:], 0.0), (), ["identf"])
        self.P.op("pool", lambda e: e.affine_select(out=self.identf[:], in_=self.identf[:], pattern=[[-1, 128]],
                                                    compare_op=ALU.not_equal, fill=1.0, base=0,
                                                    channel_multiplier=1), ["identf"], ["identf"])
        self.copy(self.ident[:], self.identf[:], ["identf"], ["ident"])
        self.P.op("pool", lambda e: e.memset(self.epsb[:], EPS), (), ["epsb"])
        self.oneb = self.sb("oneb", [128, 1], F32)
        self.P.op("pool", lambda e: e.memset(self.oneb[:], 1.0), (), ["oneb"])

    def load_weight(self, name, dst, src, nk, ncols, gain=None, stg=None, col0=0, colgain=None):
        CH = stg[0].shape[1]
        for c in range(nk):
            for o in range(0, ncols, CH):
                n = min(CH, ncols - o)
                i = self.uid % 2
                self.uid += 1
                s = stg[i]
                self.dma(s[:, 0:n], src[c * 128:(c + 1) * 128, o:o + n], w=["stg%d" % i])
                if gain is not None:
                    self.act(dst[:, c, col0 + o:col0 + o + n], s[:, 0:n], AF.Copy, ["stg%d" % i, "gains"], [name],
                             scale=gain[:, c:c + 1])
                else:
                    self.copy(dst[:, c, col0 + o:col0 + o + n], s[:, 0:n], ["stg%d" % i], [name], eng="act")

    def xnorm_T(self, xsrc, nt, par, hT_out, hT_key, nk=8, ssinv=1.0 / D):
        junk, ssx, hb = self.junk, self.ssx[par], self.hb[par]
        kx = xsrc[1]
        x = xsrc[0]
        ncol = nk * 128
        self.act(junk[:nt, 0:ncol], x, AF.Square, [kx], ["junk", "ssx%d" % par], accum=ssx[:nt, 0:1])
        self.rstd(ssx[:nt, 1:2], ssx[:nt, 0:1], ssinv, "ssx%d" % par, "rsx%d" % par, self.epsb[:nt, 0:1])
        self.act(hb[:nt, 0:ncol], x, AF.Copy, [kx, "rsx%d" % par], ["hb%d" % par], scale=ssx[:nt, 1:2])
        pb, pk = self.bank()
        pv = pb[:].bitcast(BF16)
        for c in range(nk):
            self.tr(pv[:, c * 128:c * 128 + nt], hb[:nt, c * 128:(c + 1) * 128], self.ident[:nt, :nt],
                    ["hb%d" % par, "ident"], [pk], inc=(c == nk - 1))
        self.copy(hT_out.rearrange("p c t -> p (c t)") if nt == 128 else hT_out,
                  pv[:, 0:nk * 128] if nt == 128 else pv[:, 0:nk * 128].rearrange("p (c t) -> p c t", t=128)[:, :, 0:nt],
                  [pk], [hT_key])


def build(TQ):
    gst = contextlib.ExitStack()
    with gst:
        return _build(TQ, gst)


def _build(TQ, gst):
    import os
    NPH = int(os.environ.get("NPHASE", "99"))
    nc = bass.Bass("TRN2", target_bir_lowering=False)
    NS = len(TQ)
    Prog.G = {"cnt": {}, "sems": {}, "epoch": {}, "st": gst}

    def din(name, shape, dt=F32):
        return nc.dram_tensor(name, list(shape), dt, kind="ExternalInput").ap()

    def dout(name, shape, dt=F32):
        return nc.dram_tensor(name, list(shape), dt, kind="ExternalOutput").ap()

    def dscr(name, shape, dt=F32):
        return nc.dram_tensor(name, list(shape), dt, kind="Internal").ap()

    A = {}
    for s in range(NS):
        T = TQ[s]
        A["xo%d" % s] = din("xo%d" % s, [T, D])
        A["xc%d" % s] = din("xc%d" % s, [4, T, D])
        A["rco%d" % s] = din("rco%d" % s, [T, 64])
        A["rso%d" % s] = din("rso%d" % s, [T, 64])
        A["rcc%d" % s] = din("rcc%d" % s, [4, T, 64])
        A["rsc%d" % s] = din("rsc%d" % s, [4, T, 64])
        A["y%d" % s] = dout("y%d" % s, [T, D])
        nblk = 4 * T // 128 + 1
        A["KTn%d" % s] = dscr("KTn%d" % s, [H, 128, nblk * 128], BF16)
        A["KTr%d" % s] = dscr("KTr%d" % s, [H, 64, nblk * 128], BF16)
        A["Vs%d" % s] = dscr("Vs%d" % s, [H, 128, nblk, 128], BF16)
        A["snap%d" % s] = dscr("snap%d" % s, [T // 128, 128, GH * GDV])
        A["QTn%d" % s] = dscr("QTn%d" % s, [H, 128, T], BF16)
        A["QTr%d" % s] = dscr("QTr%d" % s, [H, 64, T], BF16)
        A["OT%d" % s] = dscr("OT%d" % s, [H, 128, T], BF16)
        A["sf%d" % s] = dscr("sf%d" % s, [128, GH * GDV])
        A["x1_%d" % s] = dscr("x1_%d" % s, [T, D])
        A["siga%d" % s] = dscr("siga%d" % s, [T, D])
        A["ybg%d" % s] = dscr("ybg%d" % s, [T, D])
    for name, shape in [("meta", [NM, D]), ("rcm", [NM, 64]), ("rsm", [NM, 64]), ("flags", [5, 128, 2]),
                        ("wa_c", [5, D, 16]), ("wa2_c", [5, 17, 512]), ("wa2_m", [2, 17, 512]),
                        ("w_ctx", [D, NCTX]), ("w_main", [D, NMAIN]),
                        ("w_uq", [QL, H * QK]), ("w_ukv", [KVL, H * 256]), ("w_o_mla", [D, D]),
                        ("w_o_gla", [D, D]), ("w_out", [D, D]), ("w_ffn_gate", [D, DFF]), ("w_ffn_up", [D, DFF]),
                        ("w_ffn_down", [DFF, D]),
                        ("attn_norm", [128, 8]), ("q_a_norm", [128, 6]), ("kv_a_norm", [128, 2]), ("q_norm", [1, QK]),
                        ("k_norm", [1, QK]), ("gla_o_norm", [128, 2]), ("ffn_norm", [128, 8]),
                        ("cmat", [128, 5, 128]), ("crhs", [128, 4])]:
        A[name] = din(name, shape)

    def gate_path(C, nt, wa2_ap, wa2_key, par):
        ab, aT, lgn = C.ab[par], C.aT[par], C.lgn[par]
        pb, pk = C.bank()
        pv = pb[:].bitcast(BF16)
        C.tr(pv[0:16, 0:nt], ab[:nt, :], C.ident[:nt, :nt], ["ab%d" % par, "ident"], [pk])
        C.copy(aT[0:16, 0:nt], pv[0:16, 0:nt], [pk], ["aT%d" % par])
        pb2, pk2 = C.bank()
        C.mm(pb2[:nt, :], aT[0:17, 0:nt], wa2_ap, True, True, ["aT%d" % par, wa2_key], [pk2])
        C.act(lgn[:nt, :], pb2[:nt, :], AF.Exp, [pk2], ["lgn%d" % par], scale=-1.0)
        C.act(lgn[:nt, :], lgn[:nt, :], AF.Ln, ["lgn%d" % par, "oneb"], ["lgn%d" % par], scale=1.0, bias=C.oneb[:nt, 0:1])
        return lgn, "lgn%d" % par

    with contextlib.ExitStack() as st:
        C = Ctx(nc, st)
        C.load_consts(A)
        stg = [C.sb("stg%d" % i, [128, 2048], F32) for i in range(2)]
        gains = C.sb("gains", [128, 16], F32)
        C.dma(gains[:, 0:8], A["attn_norm"], w=["gains"])
        C.dma(gains[:, 8:10], A["kv_a_norm"], w=["gains"])
        wctx = C.sb("wctx", [128, 8, NCTX + 80], BF16)
        wukv = C.sb("wukv", [128, 2, 2048], BF16)
        C.load_weight("wctx", wctx, A["w_ctx"], 8, NCTX, gain=gains[:, 0:8], stg=stg)
        for sl in range(5):
            C.load_weight("wctx", wctx, A["wa_c"][sl], 8, 16, gain=gains[:, 0:8], stg=stg, col0=NCTX + 16 * sl)
        C.load_weight("wukv", wukv, A["w_ukv"], 2, 2048, gain=gains[:, 8:10], stg=stg)
        wa2f = C.sb("wa2f", [17, 5, 512], F32)
        wa2 = C.sb("wa2", [17, 5, 512], BF16)
        C.dma(wa2f[:], A["wa2_c"].rearrange("s k n -> k s n"), w=["wa2f"])
        C.copy(wa2[:], wa2f[:], ["wa2f"], ["wa2"])
        gk = C.sb("gk", [128, QK], F32)
        C.dma(gk[:], A["k_norm"].partition_broadcast(128), w=["gk"])
        cmat = C.sb("cmat", [128, 5, 128], F32)
        crhs = C.sb("crhs", [128, 4], F32)
        C.dma(cmat[:], A["cmat"], w=["cmat"])
        C.dma(crhs[:], A["crhs"], w=["crhs"])
        flg = C.sb("flg", [128, 5, 2], F32)
        nflg = C.sb("nflg", [128, 5, 2], F32)
        C.dma(flg[:], A["flags"].rearrange("s p t -> p s t"), w=["flg"])
        C.ts(nflg[:], flg[:], -1.0 / 16.0, ALU.mult, ["flg"], ["nflg"])
        C.junk = C.sb("junk", [128, 2048], F32)
        C.ssx = [C.sb("ssx%d" % i, [128, 4], F32) for i in range(2)]
        C.hb = [C.sb("hb%d" % i, [128, D], BF16) for i in range(2)]
        C.ab = [C.sb("ab%d" % i, [128, 16], BF16) for i in range(2)]
        C.aT = [C.sb("aT%d" % i, [32, 128], BF16) for i in range(2)]
        C.lgn = [C.sb("lgn%d" % i, [128, 512], F32) for i in range(2)]
        for i in range(2):
            C.memset(C.aT[i][:], 1.0, ["aT%d" % i])
        xt = [C.sb("xt%d" % i, [128, D], F32) for i in range(2)]
        hT = [C.sb("hT%d" % i, [128, 8, 128], BF16) for i in range(2)]
        rc = [C.sb("rc%d" % i, [128, 64], F32) for i in range(2)]
        rs = [C.sb("rs%d" % i, [128, 64], F32) for i in range(2)]
        ckb = [C.sb("ckb%d" % i, [128, 256], BF16) for i in range(2)]
        ckT = [C.sb("ckT%d" % i, [128, 2, 128], BF16) for i in range(2)]
        kfull = [C.sb("kfull%d" % i, [128, H, QK], F32) for i in range(2)]
        kn32 = C.sb("kn32", [128, H, QK], F32)
        kr32 = C.sb("kr32", [128, H, 64], F32)
        t1 = C.sb("t1", [128, H, 64], F32)
        t2 = C.sb("t2", [128, H, 64], F32)
        knb = [C.sb("knb%d" % i, [128, H, QK], BF16) for i in range(2)]
        vsb = [C.sb("vsb%d" % i, [128, H, 128], BF16) for i in range(2)]
        ssk = [C.sb("ssk%d" % i, [128, 2 * H + 2], F32) for i in range(2)]
        ktn = [C.sb("ktn%d" % i, [128, H, 128], BF16) for i in range(2)]
        ktr = [C.sb("ktr%d" % i, [64, H, 128], BF16) for i in range(2)]
        k32 = [C.sb("k32_%d" % i, [128, 512], F32) for i in range(2)]
        gvb = [C.sb("gvb%d" % i, [128, 1024], BF16) for i in range(2)]
        Eb = [C.sb("Eb%d" % i, [128, 512], F32) for i in range(2)]
        kef = [C.sb("kef%d" % i, [128, 512], BF16) for i in range(2)]
        keb = [C.sb("keb%d" % i, [128, 512], BF16) for i in range(2)]
        dec = [C.sb("dec%d" % i, [128, 2, GH], F32) for i in range(2)]
        Sf = C.sb("Sf", [128, GH, GDV], F32)
        Sb = C.sb("Sb", [128, GH, GDV], F32)
        snapb = [C.sb("snapb%d" % i, [128, GH, GDV], F32) for i in range(2)]

        tile_no = [0]

        def ctx_loads(n, s, xsrc, rcsrc, rssrc, nt, sl, keyblk, snap_idx):
            par = n % 2
            p = "%d" % par
            C.dma(xt[par][:nt, :], xsrc, w=["xt" + p])
            C.dma(rc[par][:nt, :], rcsrc, w=["rc" + p])
            C.dma(rs[par][:nt, :], rssrc, w=["rs" + p])

        def ctx_tile(n, s, xsrc, rcsrc, rssrc, nt, sl, keyblk, snap_idx):
            par = n % 2
            tile_no[0] = n + 1
            p = "%d" % par
            C.xnorm_T((xt[par][:nt, :], "xt" + p), nt, par, hT[par][:, :, 0:nt], "hT" + p)
            if CUT <= 1:
                return
            groups = [(0, 320), (320, 512), (832, 512), (1344, 512), (NCTX + 16 * sl, 16)]
            zb = []
            for (c0, n) in groups:
                pb, pk = C.bank()
                for c in range(8):
                    C.mm(pb[:nt, 0:n], hT[par][:, c, 0:nt], wctx[:, c, c0:c0 + n], c == 0, c == 7,
                         ["hT" + p, "wctx"], [pk], inc=(c == 7))
                zb.append((pb, pk))
            (pkv, kkv), (pgk, kgk), (pv0, kv0), (pv1, kv1), (pa, ka) = zb
            C.copy(k32[par][:nt, :], pgk[:nt, :], [kgk], ["k32_" + p], eng="act")
            C.copy(gvb[par][:nt, 0:512], pv0[:nt, :], [kv0], ["gvb" + p], eng="act")
            C.copy(gvb[par][:nt, 512:1024], pv1[:nt, :], [kv1], ["gvb" + p], eng="dve")
            C.copy(C.ab[par][:nt, :], pa[:nt, 0:16], [ka], ["ab" + p])
            if CUT <= 2:
                return
            C.act(C.junk[:nt, 0:256], pkv[:nt, 0:256], AF.Square, [kkv], ["junk", "ssk" + p],
                  accum=ssk[par][:nt, 16:17])
            C.rstd(ssk[par][:nt, 17:18], ssk[par][:nt, 16:17], 1.0 / KVL, "ssk" + p, "ssk" + p, C.epsb[:nt, 0:1])
            C.act(ckb[par][:nt, :], pkv[:nt, 0:256], AF.Copy, [kkv, "ssk" + p], ["ckb" + p], scale=ssk[par][:nt, 17:18])
            pb, pk = C.bank()
            pvw = pb[:].bitcast(BF16)
            for c in range(2):
                C.tr(pvw[:, c * 128:c * 128 + nt], ckb[par][:nt, c * 128:(c + 1) * 128], C.ident[:nt, :nt],
                     ["ckb" + p, "ident"], [pk], inc=(c == 1))
            C.copy(ckT[par][:, :, 0:nt], pvw[:, 0:256].rearrange("p (c t) -> p c t", t=128)[:, :, 0:nt], [pk], ["ckT" + p])
            C.copy(kfull[par][:nt, :, 128:192], pkv[:nt, 256:320].unsqueeze(1).to_broadcast([nt, H, 64]), [kkv],
                   ["kfull" + p])
            if CUT <= 3:
                return
            for g in range(4):
                pb, pk = C.bank()
                for c in range(2):
                    C.mm(pb[:nt, :], ckT[par][:, c, 0:nt], wukv[:, c, g * 512:(g + 1) * 512], c == 0, c == 1,
                         ["ckT" + p, "wukv"], [pk], inc=(c == 1))
                pv3 = pb[:nt, :].rearrange("p (h x) -> p h x", x=256)
                C.copy(kfull[par][:nt, 2 * g:2 * g + 2, 0:128], pv3[:, :, 0:128], [pk], ["kfull" + p], eng="act")
                C.copy(vsb[par][:nt, 2 * g:2 * g + 2, :], pv3[:, :, 128:256], [pk], ["vsb" + p], eng="dve")
            if CUT <= 4:
                return
            C.act(kn32[:nt], kfull[par][:nt], AF.Square, ["kfull" + p], ["kn32"])
            C.red(ssk[par][:nt, 0:H], kn32[:nt], ["kn32"], ["ssk" + p])
            C.rstd(ssk[par][:nt, H:2 * H], ssk[par][:nt, 0:H], 1.0 / QK, "ssk" + p, "ssk" + p, C.epsb[:nt, 0:1])
            C.tt(kn32[:nt], kfull[par][:nt], ssk[par][:nt, H:2 * H].unsqueeze(2).to_broadcast([nt, H, QK]), ALU.mult,
                 ["kfull" + p, "ssk" + p], ["kn32"])
            C.tt(knb[par][:nt, :, 0:128], kn32[:nt, :, 0:128], gk[:nt, 0:128].unsqueeze(1).to_broadcast([nt, H, 128]),
                 ALU.mult, ["kn32", "gk"], ["knb" + p])
            C.tt(kr32[:nt], kn32[:nt, :, 128:192], gk[:nt, 128:192].unsqueeze(1).to_broadcast([nt, H, 64]), ALU.mult,
                 ["kn32", "gk"], ["kr32"])
            C.tt(t1[:nt], kr32[:nt], rc[par][:nt, :].unsqueeze(1).to_broadcast([nt, H, 64]), ALU.mult,
                 ["kr32", "rc" + p], ["t1"])
            C.tt(t2[:nt, :, 0:32], kr32[:nt, :, 32:64], rs[par][:nt, 0:32].unsqueeze(1).to_broadcast([nt, H, 32]),
                 ALU.mult, ["kr32", "rs" + p], ["t2"])
            C.tt(t2[:nt, :, 32:64], kr32[:nt, :, 0:32], rs[par][:nt, 32:64].unsqueeze(1).to_broadcast([nt, H, 32]),
                 ALU.mult, ["kr32", "rs" + p], ["t2"])
            C.tt(knb[par][:nt, :, 128:192], t1[:nt], t2[:nt], ALU.add, ["t1", "t2"], ["knb" + p])
            if CUT <= 5:
                return
            pb, pk = C.bank()
            pvn = pb[:].bitcast(BF16)
            pb2, pk2 = C.bank()
            pvr = pb2[:].bitcast(BF16)
            for h in range(H):
                C.tr(pvn[:, h * 128:h * 128 + nt], knb[par][:nt, h, 0:128], C.ident[:nt, :nt], ["knb" + p, "ident"], [pk],
                     inc=False)
                C.tr(pvr[0:64, h * 128:h * 128 + nt], knb[par][:nt, h, 128:192], C.ident[:nt, :nt], ["knb" + p, "ident"],
                     [pk2], inc=(h == H - 1))
            C.copy(ktn[par][:, :, 0:nt], pvn[:, :].rearrange("p (h t) -> p h t", t=128)[:, :, 0:nt], [pk], ["ktn" + p],
                   eng="act")
            C.copy(ktr[par][:, :, 0:nt], pvr[0:64, :].rearrange("p (h t) -> p h t", t=128)[:, :, 0:nt], [pk2], ["ktr" + p],
                   eng="dve")
            k0 = keyblk * 128
            C.dma(A["KTn%d" % s][:, :, k0:k0 + nt].rearrange("h d t -> d h t"), ktn[par][:, :, 0:nt], r=["ktn" + p], q=STQ)
            C.dma(A["KTr%d" % s][:, :, k0:k0 + nt].rearrange("h d t -> d h t"), ktr[par][:, :, 0:nt], r=["ktr" + p], q=STQ)
            C.dma(A["Vs%d" % s][:, 0:nt, keyblk, :].rearrange("h p v -> p h v"), vsb[par][:nt], r=["vsb" + p], q=STQ)
            if CUT <= 6:
                return
            lgn, klg = gate_path(C, nt, wa2[0:17, sl, :], "wa2", par)
            pb, pk = C.bank()
            C.mm(pb[:nt, :], cmat[:nt, 0, 0:nt], lgn[:nt, :], True, True, ["cmat", klg], [pk])
            C.act(Eb[par][:nt, :], pb[:nt, :], AF.Exp, [pk], ["Eb" + p], scale=-1.0 / 16.0)
            fwd_only = (sl == 4)
            bwd_only = (sl == 3)
            if not bwd_only:
                C.stt(kef[par][:nt, :], k32[par][:nt, :], flg[:nt, sl, 0:1], Eb[par][:nt, :], ALU.mult, ALU.mult,
                      ["k32_" + p, "flg", "Eb" + p], ["kef" + p])
            if not fwd_only:
                C.stt(keb[par][:nt, :], k32[par][:nt, :], flg[:nt, sl, 1:2], Eb[par][:nt, :], ALU.mult, ALU.mult,
                      ["k32_" + p, "flg", "Eb" + p], ["keb" + p])
            if CUT <= 7:
                return
            pb, pk = C.bank()
            for h in range(GH):
                C.mm(pb[:, 2 * h:2 * h + 2], lgn[:nt, h * 128:(h + 1) * 128], crhs[:nt, 0:2], True, True,
                     [klg, "crhs"], [pk], inc=(h == GH - 1))
            pdv = pb[:, 0:8].rearrange("p (h t) -> p h t", t=2)[:, :, 0]
            if not bwd_only:
                C.act(dec[par][:, 0, :], pdv, AF.Exp, [pk, "nflg"], ["dec" + p], scale=nflg[:, sl, 0:1])
            if not fwd_only:
                C.act(dec[par][:, 1, :], pdv, AF.Exp, [pk, "nflg"], ["dec" + p], scale=nflg[:, sl, 1:2])
            if snap_idx is not None:
                sp_ = tile_no[0] % 2
                C.copy(snapb[sp_][:], Sb[:], ["Sb"], ["snapb%d" % sp_], eng="act")
                C.dma(A["snap%d" % s][snap_idx].rearrange("p (h v) -> p h v", v=GDV), snapb[sp_][:], r=["snapb%d" % sp_],
                      q=STQ)
            for (skip, ke, kek, S, Sk, di) in [(bwd_only, kef, "kef", Sf, "Sf", 0), (fwd_only, keb, "keb", Sb, "Sb", 1)]:
                if skip:
                    continue
                for hp in range(2):
                    pb, pk = C.bank()
                    for hh in range(2):
                        h = 2 * hp + hh
                        C.mm(pb[:, hh * 256:(hh + 1) * 256], ke[par][:nt, h * 128:(h + 1) * 128],
                             gvb[par][:nt, h * 256:(h + 1) * 256], True, True, [kek + p, "gvb" + p], [pk], inc=(hh == 1))
                    for hh in range(2):
                        h = 2 * hp + hh
                        C.stt(S[:, h, :], S[:, h, :], dec[par][:, di, h:h + 1], pb[:, hh * 256:(hh + 1) * 256], ALU.mult,
                              ALU.add, [Sk, "dec" + p, pk], [Sk])

        tl = []
        for s in range(NS):
            T = TQ[s]
            ntl = T // 128
            tl.append(("first", (s, A["meta"], A["rcm"], A["rsm"], NM, 4, 4 * ntl, None)))
            for sl in range(4):
                for t in range(min(ntl, TLIM)):
                    kb = (t if sl == 3 else ntl * (sl + 1) + t)
                    tl.append(("", (s, A["xc%d" % s][sl, t * 128:(t + 1) * 128, :],
                                    A["rcc%d" % s][sl, t * 128:(t + 1) * 128, :],
                                    A["rsc%d" % s][sl, t * 128:(t + 1) * 128, :], 128, sl, kb,
                                    (ntl - 1 - t) if sl == 3 else None)))
            tl[-1] = ("last" if tl[-1][0] == "" else "firstlast", tl[-1][1])
        ctx_loads(0, *tl[0][1])
        for n, (tag, args) in enumerate(tl):
            if n + 1 < len(tl):
                ctx_loads(n + 1, *tl[n + 1][1])
            if "first" in tag:
                C.memset(Sf[:], 0.0, ["Sf"])
                C.memset(Sb[:], 0.0, ["Sb"])
            ctx_tile(n, *args)
            if "last" in tag:
                s = args[0]
                C.copy(snapb[0][:], Sf[:], ["Sf"], ["snapb0"], eng="act")
                C.dma(A["sf%d" % s].rearrange("p (h v) -> p h v", v=GDV), snapb[0][:], r=["snapb0"], q=STQ)
        C.P.build()

    def phase2(part):
      with contextlib.ExitStack() as st:
            C = Ctx(nc, st)
            C.nrot = 6
            C.load_consts(A)
            stg = [C.sb("stg%d" % i, [128, 1024], F32) for i in range(2)]
            gains = C.sb("gains", [128, 16], F32)
            C.dma(gains[:, 0:8], A["attn_norm"], w=["gains"])
            C.dma(gains[:, 8:14], A["q_a_norm"], w=["gains"])
            gon = C.sb("gon", [128, 2], F32)
            C.dma(gon[:], A["gla_o_norm"], w=["gon"])
            gon8 = C.sb("gon8", [128, 8], F32)
            C.copy(gon8[:].rearrange("p (a b) -> p a b", b=2), gon[:].unsqueeze(1).to_broadcast([128, 4, 2]), ["gon"], ["gon8"])
            if part == "q":
                wmain = C.sb("wmain", [128, 8, 768], BF16)
                wuq = C.sb("wuq", [128, 6, H * QK], BF16)
                C.load_weight("wmain", wmain, A["w_main"][:, 0:768], 8, 768, gain=gains[:, 0:8], stg=stg)
                C.load_weight("wuq", wuq, A["w_uq"], 6, H * QK, gain=gains[:, 8:14], stg=stg)
            else:
                wmain = C.sb("wmain", [128, 8, NMAIN - 768], BF16)
                wogla = C.sb("wogla", [128, 8, D], BF16)
                C.load_weight("wmain", wmain, A["w_main"][:, 768:NMAIN], 8, NMAIN - 768, gain=gains[:, 0:8], stg=stg)
                C.load_weight("wogla", wogla, A["w_o_gla"], 8, D, gain=gon8[:, 0:8], stg=stg)
            wa2f = C.sb("wa2f", [17, 2, 512], F32)
            wa2 = C.sb("wa2", [17, 2, 512], BF16)
            C.dma(wa2f[:], A["wa2_m"].rearrange("s k n -> k s n"), w=["wa2f"])
            C.copy(wa2[:], wa2f[:], ["wa2f"], ["wa2"])
            gq = C.sb("gq", [128, QK], F32)
            C.dma(gq[:], A["q_norm"].partition_broadcast(128), w=["gq"])
            C.ts(gq[:], gq[:], float(QK) ** -0.5, ALU.mult, ["gq"], ["gq"])
            cmat = C.sb("cmat", [128, 5, 128], F32)
            crhs = C.sb("crhs", [128, 4], F32)
            C.dma(cmat[:], A["cmat"], w=["cmat"])
            C.dma(crhs[:], A["crhs"], w=["crhs"])
            C.junk = C.sb("junk", [128, 1024], F32)
            C.ssx = [C.sb("ssx%d" % i, [128, 4], F32) for i in range(2)]
            C.hb = [C.sb("hb%d" % i, [128, D], BF16) for i in range(2)]
            C.ab = [C.sb("ab%d" % i, [128, 16], BF16) for i in range(2)]
            C.aT = [C.sb("aT%d" % i, [32, 128], BF16) for i in range(2)]
            C.lgn = [C.sb("lgn%d" % i, [128, 512], F32) for i in range(2)]
            for i in range(2):
                C.memset(C.aT[i][:], 1.0, ["aT%d" % i])
            xt2 = [C.sb("xt%d" % i, [128, D], F32) for i in range(2)]
            hT = C.sb("hT", [128, 8, 128], BF16)
            rc2 = [C.sb("rc%d" % i, [128, 64], F32) for i in range(2)]
            rs2 = [C.sb("rs%d" % i, [128, 64], F32) for i in range(2)]
            if part == "q":
                cqT = C.sb("cqT", [128, 6, 128], BF16)
                qfull = C.sb("qfull", [128, H, QK], F32)
                qn32 = C.sb("qn32", [128, H, QK], F32)
                qr32 = C.sb("qr32", [128, H, 64], F32)
                t1 = C.sb("t1", [128, H, 64], F32)
                t2 = C.sb("t2", [128, H, 64], F32)
                qb = C.sb("qb", [128, H, QK], BF16)
                ssq = C.sb("ssq", [128, 2 * H + 2], F32)
                qtn = C.sb("qtn", [128, H, 128], BF16)
                qtr = C.sb("qtr", [64, H, 128], BF16)
            if part == "gla":
                q32 = C.sb("q32", [128, 512], F32)
                k32 = C.sb("k32", [128, 512], F32)
                gvb = C.sb("gvb", [128, 1024], BF16)
                Eq = C.sb("Eq", [128, 512], F32)
                Ek = C.sb("Ek", [128, 512], F32)
                Eb = C.sb("Eb", [128, 512], F32)
                qd = C.sb("qd", [128, 512], BF16)
                kd = C.sb("kd", [128, 512], BF16)
                ke = C.sb("ke", [128, 512], BF16)
                qdT = [C.sb("qdT%d" % i, [128, GH, 128], BF16) for i in range(2)]
                kdT = [C.sb("kdT%d" % i, [128, GH, 128], BF16) for i in range(2)]
                atb = [C.sb("atb%d" % i, [128, GH, 128], BF16) for i in range(2)]
                dec = C.sb("dec", [128, 2, GH, 2], F32)
                Sf = C.sb("Sf", [128, GH, GDV], F32)
                Sbl2 = [C.sb("Sbl%d" % i, [128, GH, GDV], F32) for i in range(2)]
                Sp = [C.sb("Sp%d" % i, [128, GH, GDV], BF16) for i in range(2)]
                o32 = C.sb("o32", [128, GH, GDV], F32)
                sso = C.sb("sso", [128, 2 * GH], F32)
                sg = C.sb("sg", [128, D], F32)
                ogb = C.sb("ogb", [128, D], BF16)
                ogT = C.sb("ogT", [128, 8, 128], BF16)
                siga = C.sb("siga", [128, D], F32)
                sigb = C.sb("sigb", [128, D], F32)
                ybg = C.sb("ybg", [128, D], F32)
            O_CQ = 0
            O_GQ, O_GK, O_GV, O_GG, O_AF, O_GA, O_GB = 0, 512, 1024, 2048, 3072, 3104, 4128

            def zmm(c0, n, nt=128):
                pb, pk = C.bank()
                for c in range(8):
                    C.mm(pb[:nt, 0:n], hT[:, c, 0:nt], wmain[:, c, c0:c0 + n], c == 0, c == 7, ["hT", "wmain"], [pk],
                         inc=(c == 7))
                return pb, pk

            tiles = [(s, t) for s in range(NS) for t in range(TQ[s] // 128)]

            def p2_loads(n):
                s, t = tiles[n]
                par = n % 2
                tok = slice(t * 128, (t + 1) * 128)
                C.dma(xt2[par][:], A["xo%d" % s][tok, :], w=["xt%d" % par])
                C.dma(rc2[par][:], A["rco%d" % s][tok, :], w=["rc%d" % par])
                C.dma(rs2[par][:], A["rso%d" % s][tok, :], w=["rs%d" % par])
                if part == "gla":
                    C.dma(Sbl2[par][:], A["snap%d" % s][t].rearrange("p (h v) -> p h v", v=GDV), w=["Sbl%d" % par])

            p2_loads(0)
            for n, (s, t) in enumerate(tiles):
                if True:
                    if n + 1 < len(tiles):
                        p2_loads(n + 1)
                    par = n % 2
                    xt, rc, rs = xt2[par], rc2[par], rs2[par]
                    kxt, krc, krs, kSbl = "xt%d" % par, "rc%d" % par, "rs%d" % par, "Sbl%d" % par
                    if part == "gla":
                        Sbl = Sbl2[par]
                        if t == 0:
                            C.dma(Sf[:], A["sf%d" % s].rearrange("p (h v) -> p h v", v=GDV), w=["Sf"])
                    nt = 128
                    tok = slice(t * 128, (t + 1) * 128)
                    C.xnorm_T((xt[:], kxt), 128, 0, hT[:], "hT")
                    if part == "q":
                        pq0, kq0 = zmm(O_CQ, 512)
                        pq1, kq1 = zmm(O_CQ + 512, 256)
                        C.act(C.junk[:, 0:512], pq0[:, :], AF.Square, [kq0], ["junk", "ssq"], accum=ssq[:, 16:17])
                        C.act(C.junk[:, 512:768], pq1[:, 0:256], AF.Square, [kq1], ["junk", "ssq"], accum=ssq[:, 17:18])
                        C.tt(ssq[:, 16:17], ssq[:, 16:17], ssq[:, 17:18], ALU.add, ["ssq"], ["ssq"])
                        C.rstd(ssq[:, 17:18], ssq[:, 16:17], 1.0 / QL, "ssq", "ssq", C.epsb[:, 0:1])
                        cqb = C.hb[1]
                        C.act(cqb[:, 0:512], pq0[:, :], AF.Copy, [kq0, "ssq"], ["cqb"], scale=ssq[:, 17:18])
                        C.act(cqb[:, 512:768], pq1[:, 0:256], AF.Copy, [kq1, "ssq"], ["cqb"], scale=ssq[:, 17:18])
                        pb, pk = C.bank()
                        pvw = pb[:].bitcast(BF16)
                        for c in range(6):
                            C.tr(pvw[:, c * 128:(c + 1) * 128], cqb[:, c * 128:(c + 1) * 128], C.ident[:], ["cqb", "ident"], [pk],
                                 inc=(c == 5))
                        C.copy(cqT[:].rearrange("p c t -> p (c t)"), pvw[:, 0:768], [pk], ["cqT"])
                        for g in range(3):
                            pb, pk = C.bank()
                            for c in range(6):
                                C.mm(pb[:, :], cqT[:, c, :], wuq[:, c, g * 512:(g + 1) * 512], c == 0, c == 5, ["cqT", "wuq"], [pk],
                                     inc=(c == 5))
                            C.copy(qfull[:].rearrange("p h x -> p (h x)")[:, g * 512:(g + 1) * 512], pb[:, :], [pk], ["qfull"],
                                   eng="act")
                        C.act(qn32[:], qfull[:], AF.Square, ["qfull"], ["qn32"])
                        C.red(ssq[:, 0:H], qn32[:], ["qn32"], ["ssq"])
                        C.rstd(ssq[:, H:2 * H], ssq[:, 0:H], 1.0 / QK, "ssq", "ssq", C.epsb[:, 0:1])
                        C.tt(qn32[:], qfull[:], ssq[:, H:2 * H].unsqueeze(2).to_broadcast([128, H, QK]), ALU.mult, ["qfull", "ssq"],
                             ["qn32"])
                        C.tt(qb[:, :, 0:128], qn32[:, :, 0:128], gq[:, 0:128].unsqueeze(1).to_broadcast([128, H, 128]), ALU.mult,
                             ["qn32", "gq"], ["qb"])
                        C.tt(qr32[:], qn32[:, :, 128:192], gq[:, 128:192].unsqueeze(1).to_broadcast([128, H, 64]), ALU.mult,
                             ["qn32", "gq"], ["qr32"])
                        C.tt(t1[:], qr32[:], rc[:].unsqueeze(1).to_broadcast([128, H, 64]), ALU.mult, ["qr32", krc], ["t1"])
                        C.tt(t2[:, :, 0:32], qr32[:, :, 32:64], rs[:, 0:32].unsqueeze(1).to_broadcast([128, H, 32]), ALU.mult,
                             ["qr32", krs], ["t2"])
                        C.tt(t2[:, :, 32:64], qr32[:, :, 0:32], rs[:, 32:64].unsqueeze(1).to_broadcast([128, H, 32]), ALU.mult,
                             ["qr32", krs], ["t2"])
                        C.tt(qb[:, :, 128:192], t1[:], t2[:], ALU.add, ["t1", "t2"], ["qb"])
                        pb, pk = C.bank()
                        pvn = pb[:].bitcast(BF16)
                        pb2, pk2 = C.bank()
                        pvr = pb2[:].bitcast(BF16)
                        for h in range(H):
                            C.tr(pvn[:, h * 128:(h + 1) * 128], qb[:, h, 0:128], C.ident[:], ["qb", "ident"], [pk], inc=False)
                            C.tr(pvr[0:64, h * 128:(h + 1) * 128], qb[:, h, 128:192], C.ident[:], ["qb", "ident"], [pk2],
                                 inc=(h == H - 1))
                        C.copy(qtn[:].rearrange("p h t -> p (h t)"), pvn[:, :], [pk], ["qtn"], eng="act")
                        C.copy(qtr[:].rearrange("p h t -> p (h t)"), pvr[0:64, :], [pk2], ["qtr"], eng="dve")
                        C.dma(A["QTn%d" % s][:, :, tok].rearrange("h d t -> d h t"), qtn[:], r=["qtn"], q=STQ)
                        C.dma(A["QTr%d" % s][:, :, tok].rearrange("h d t -> d h t"), qtr[:], r=["qtr"], q=STQ)
                    if part == "gla":
                        pgq, kgq = zmm(O_GQ, 512)
                        pgk, kgk = zmm(O_GK, 512)
                        pv0, kv0 = zmm(O_GV, 512)
                        pv1, kv1 = zmm(O_GV + 512, 512)
                        pa, ka = zmm(O_AF, 32)
                        C.act(q32[:], pgq[:, :], AF.Copy, [kgq], ["q32"], scale=float(GDK) ** -0.5)
                        C.copy(k32[:], pgk[:, :], [kgk], ["k32"], eng="act")
                        C.copy(gvb[:, 0:512], pv0[:, :], [kv0], ["gvb"], eng="dve")
                        C.copy(gvb[:, 512:1024], pv1[:, :], [kv1], ["gvb"], eng="dve")
                        lgns = []
                        for di in range(2):
                            C.copy(C.ab[di][:, :], pa[:, 16 * di:16 * di + 16], [ka], ["ab%d" % di])
                        for di in range(2):
                            lgns.append(gate_path(C, 128, wa2[0:17, di, :], "wa2", di))
                        ob = [(C.banks[6], "pb6"), (C.banks[7], "pb7")]
                        for di in range(2):
                            lgn, klg = lgns[di]
                            pb, pk = C.bank()
                            C.mm(pb[:, :], cmat[:, 1 + 2 * di, :], lgn[:, :], True, True, ["cmat", klg], [pk])
                            C.act(Eq[:], pb[:, :], AF.Exp, [pk], ["Eq"], scale=-1.0 / 16.0)
                            C.act(Ek[:], pb[:, :], AF.Exp, [pk], ["Ek"], scale=1.0 / 16.0)
                            C.tt(qd[:], q32[:], Eq[:], ALU.mult, ["q32", "Eq"], ["qd"])
                            C.tt(kd[:], k32[:], Ek[:], ALU.mult, ["k32", "Ek"], ["kd"])
                            pb, pk = C.bank()
                            pvw = pb[:].bitcast(BF16)
                            for h in range(GH):
                                C.tr(pvw[:, h * 128:(h + 1) * 128], qd[:, h * 128:(h + 1) * 128], C.ident[:], ["qd", "ident"], [pk],
                                     inc=False)
                                C.tr(pvw[:, 512 + h * 128:512 + (h + 1) * 128], kd[:, h * 128:(h + 1) * 128], C.ident[:],
                                     ["kd", "ident"], [pk], inc=(h == GH - 1))
                            C.copy(qdT[di][:].rearrange("p h t -> p (h t)"), pvw[:, 0:512], [pk], ["qdT%d" % di], eng="act")
                            C.copy(kdT[di][:].rearrange("p h t -> p (h t)"), pvw[:, 512:1024], [pk], ["kdT%d" % di], eng="dve")
                            pb, pk = C.bank()
                            for h in range(GH):
                                C.mm(pb[:, h * 128:(h + 1) * 128], kdT[di][:, h, :], qdT[di][:, h, :], True, True,
                                     ["kdT%d" % di, "qdT%d" % di], [pk], inc=(h == GH - 1))
                            C.tt(atb[di][:], pb[:, :].rearrange("p (h t) -> p h t", t=128),
                                 cmat[:, 2 + 2 * di, :].unsqueeze(1).to_broadcast([128, GH, 128]), ALU.mult, [pk, "cmat"],
                                 ["atb%d" % di])
                            pb, pk = C.bank()
                            for h in range(GH):
                                C.mm(pb[:, 2 * h:2 * h + 2], lgn[:, h * 128:(h + 1) * 128], crhs[:, 2 * di:2 * di + 2], True, True,
                                     [klg, "crhs"], [pk], inc=(h == GH - 1))
                            C.act(dec[:, di].rearrange("p h t -> p (h t)"), pb[:, 0:8], AF.Exp, [pk], ["dec"], scale=-1.0 / 16.0)
                            S = Sf if di == 0 else Sbl
                            Sk = "Sf" if di == 0 else kSbl
                            for h in range(GH):
                                C.ts(Sp[di][:, h, :], S[:, h, :], dec[:, di, h, 1:2], ALU.mult, [Sk, "dec"], ["Sp%d" % di])
                        for di in range(2):
                            for h in range(GH):
                                pb, pk = ob[h // 2]
                                osl = pb[:, (h % 2) * 256:(h % 2 + 1) * 256]
                                C.mm(osl, atb[di][:, h, :], gvb[:, h * 256:(h + 1) * 256], (di == 0 and h % 2 == 0), False,
                                     ["atb%d" % di, "gvb"], [pk], inc=False, sgc=True)
                                C.mm(osl, qdT[di][:, h, :], Sp[di][:, h, :], False, di == 1, ["qdT%d" % di, "Sp%d" % di], [pk],
                                     inc=(di == 1 and h % 2 == 1), sgc=True)
                        lgn, klg = lgns[0]
                        pb, pk = C.bank()
                        C.mm(pb[:, :], cmat[:, 0, :], lgn[:, :], True, True, ["cmat", klg], [pk])
                        C.act(Eb[:], pb[:, :], AF.Exp, [pk], ["Eb"], scale=-1.0 / 16.0)
                        C.tt(ke[:], k32[:], Eb[:], ALU.mult, ["k32", "Eb"], ["ke"])
                        for hp in range(2):
                            pb, pk = C.bank()
                            for hh in range(2):
                                h = 2 * hp + hh
                                C.mm(pb[:, hh * 256:(hh + 1) * 256], ke[:, h * 128:(h + 1) * 128], gvb[:, h * 256:(h + 1) * 256],
                                     True, True, ["ke", "gvb"], [pk], inc=(hh == 1))
                            for hh in range(2):
                                h = 2 * hp + hh
                                C.stt(Sf[:, h, :], Sf[:, h, :], dec[:, 0, h, 0:1], pb[:, hh * 256:(hh + 1) * 256], ALU.mult, ALU.add,
                                      ["Sf", "dec", pk], ["Sf"])
                        for hp in range(2):
                            C.copy(o32[:, 2 * hp:2 * hp + 2, :].rearrange("p h v -> p (h v)"), ob[hp][0][:, :], [ob[hp][1]], ["o32"],
                                   eng="act")
                        C.act(C.junk[:, 0:1024], o32[:].rearrange("p h v -> p (h v)"), AF.Square, ["o32"], ["junk"])
                        C.red(sso[:, 0:GH], C.junk[:, 0:1024].rearrange("p (h v) -> p h v", v=GDV), ["junk"], ["sso"])
                        C.rstd(sso[:, GH:2 * GH], sso[:, 0:GH], 1.0 / GDV, "sso", "sso", C.epsb[:, 0:1])
                        pg0, kg0 = zmm(O_GG, 512)
                        pg1, kg1 = zmm(O_GG + 512, 512)
                        C.act(sg[:, 0:512], pg0[:, :], AF.Silu, [kg0], ["sg"])
                        C.act(sg[:, 512:1024], pg1[:, :], AF.Silu, [kg1], ["sg"])
                        pa0, ka0 = zmm(O_GA, 512)
                        pa1, ka1 = zmm(O_GA + 512, 512)
                        C.act(siga[:, 0:512], pa0[:, :], AF.Sigmoid, [ka0], ["siga"])
                        C.act(siga[:, 512:1024], pa1[:, :], AF.Sigmoid, [ka1], ["siga"])
                        pb0, kb0 = zmm(O_GB, 512)
                        pb1, kb1 = zmm(O_GB + 512, 512)
                        C.act(sigb[:, 0:512], pb0[:, :], AF.Sigmoid, [kb0], ["sigb"])
                        C.act(sigb[:, 512:1024], pb1[:, :], AF.Sigmoid, [kb1], ["sigb"])
                        C.dma(A["siga%d" % s][tok, :], siga[:], r=["siga"], q=STQ)
                        C.tt(o32[:], o32[:], sso[:, GH:2 * GH].unsqueeze(2).to_broadcast([128, GH, GDV]), ALU.mult, ["o32", "sso"],
                             ["o32"])
                        C.tt(ogb[:], o32[:].rearrange("p h v -> p (h v)"), sg[:], ALU.mult, ["o32", "sg"], ["ogb"])
                        pb, pk = C.bank()
                        pvw = pb[:].bitcast(BF16)
                        for c in range(8):
                            C.tr(pvw[:, c * 128:(c + 1) * 128], ogb[:, c * 128:(c + 1) * 128], C.ident[:], ["ogb", "ident"], [pk],
                                 inc=(c == 7))
                        C.copy(ogT[:].rearrange("p c t -> p (c t)"), pvw[:, :], [pk], ["ogT"])
                        for g in range(2):
                            pb, pk = C.bank()
                            for c in range(8):
                                C.mm(pb[:, :], ogT[:, c, :], wogla[:, c, g * 512:(g + 1) * 512], c == 0, c == 7, ["ogT", "wogla"],
                                     [pk], inc=(c == 7))
                            C.tt(ybg[:, g * 512:(g + 1) * 512], pb[:, :], sigb[:, g * 512:(g + 1) * 512], ALU.mult, [pk, "sigb"],
                                 ["ybg"])
                        C.dma(A["ybg%d" % s][tok, :], ybg[:], r=["ybg"], q=STQ)
            C.P.build()

    if NPH >= 2:
        phase2("q")
    if NPH >= 3:
        phase2("gla")

    with contextlib.ExitStack() as st:
      if NPH >= 4:
        C = Ctx(nc, st)
        onesb = C.sb("onesb", [128, 128], BF16)
        C.memset(onesb[:], 1.0, ["onesb"])
        KB = 16
        LA = 2
        qn_t = [C.sb("qn_t%d" % i, [128, 512], BF16) for i in range(2)]
        qr_t = [C.sb("qr_t%d" % i, [64, 512], BF16) for i in range(2)]
        kn_t = [C.sb("kn_t%d" % i, [128, KB * 128], BF16) for i in range(3)]
        kr_t = [C.sb("kr_t%d" % i, [64, KB * 128], BF16) for i in range(3)]
        v_t = [C.sb("v_t%d" % i, [128, KB, 128], BF16) for i in range(3)]
        pbuf = [C.sb("pbuf%d" % i, [128, 512], BF16) for i in range(4)]
        rden = C.sb("rden", [128, 512], F32)
        otb = [C.sb("otb%d" % i, [128, 512], BF16) for i in range(2)]
        sbanks = [(C.banks[i], "pb%d" % i) for i in range(4)]
        groups = []
        chunks = []
        blocks = []
        for s in range(NS):
            T = TQ[s]
            nfull = 4 * (T // 128)
            for q0 in range(0, T, 512):
                nq = min(512, T - q0)
                for h in range(H):
                    gi = len(groups)
                    groups.append((s, q0, nq, h))
                    nb_tot = nfull + 1
                    for cb in range(0, nb_tot, KB):
                        nb = min(KB, nb_tot - cb)
                        ck = len(chunks)
                        chunks.append((gi, cb, nb))
                        for bi in range(nb):
                            b = cb + bi
                            blocks.append((gi, ck, bi, (128 if b < nfull else NM), b == 0, b == nb_tot - 1))

        def load_q(gi):
            s, q0, nq, h = groups[gi]
            qi = gi % 2
            C.dma(qn_t[qi][:, 0:nq], A["QTn%d" % s][h, :, q0:q0 + nq], w=["qn_t%d" % qi])
            C.dma(qr_t[qi][:, 0:nq], A["QTr%d" % s][h, :, q0:q0 + nq], w=["qr_t%d" % qi])

        def load_chunk(ck):
            gi, cb, nb = chunks[ck]
            s, q0, nq, h = groups[gi]
            ci = ck % 3
            C.dma(kn_t[ci][:, 0:nb * 128], A["KTn%d" % s][h, :, cb * 128:(cb + nb) * 128], w=["kn_t%d" % ci])
            C.dma(kr_t[ci][:, 0:nb * 128], A["KTr%d" % s][h, :, cb * 128:(cb + nb) * 128], w=["kr_t%d" % ci])
            C.dma(v_t[ci][:, 0:nb, :], A["Vs%d" % s][h, :, cb:cb + nb, :], w=["v_t%d" % ci])

        def emit_S(i):
            gi, ck, bi, nk, first, last = blocks[i]
            s, q0, nq, h = groups[gi]
            qi, ci = gi % 2, ck % 3
            sb_, skey = sbanks[i % 4]
            C.mm(sb_[:nk, 0:nq], kn_t[ci][:, bi * 128:bi * 128 + nk], qn_t[qi][:, 0:nq], True, False,
                 ["kn_t%d" % ci, "qn_t%d" % qi], [skey], inc=False)
            C.mm(sb_[:nk, 0:nq], kr_t[ci][:, bi * 128:bi * 128 + nk], qr_t[qi][:, 0:nq], False, True,
                 ["kr_t%d" % ci, "qr_t%d" % qi], [skey])
            if last and gi + 2 < len(groups):
                load_q(gi + 2)

        def emit_PV(j):
            gi, ck, bi, nk, first, last = blocks[j]
            s, q0, nq, h = groups[gi]
            qi, ci, pi = gi % 2, ck % 3, j % 4
            sb_, skey = sbanks[j % 4]
            ob, okey = C.banks[4 + qi], "pb%d" % (4 + qi)
            db, dkey = C.banks[6 + qi], "pb%d" % (6 + qi)
            C.act(pbuf[pi][:nk, 0:nq], sb_[:nk, 0:nq], AF.Exp, [skey], ["pbuf%d" % pi])
            C.mm(ob[:, 0:nq], v_t[ci][:nk, bi, :], pbuf[pi][:nk, 0:nq], first, last, ["v_t%d" % ci, "pbuf%d" % pi], [okey],
                 inc=False)
            C.mm(db[:, 0:nq], onesb[:nk, :], pbuf[pi][:nk, 0:nq], first, last, ["onesb", "pbuf%d" % pi], [dkey])
            if bi == chunks[ck][2] - 1 and ck + 3 < len(chunks):
                load_chunk(ck + 3)
            if last:
                C.P.op("dve", lambda e, db=db, nq=nq: e.reciprocal(out=rden[:, 0:nq], in_=db[:, 0:nq]), [dkey], ["rden"])
                C.tt(otb[qi][:, 0:nq], ob[:, 0:nq], rden[:, 0:nq], ALU.mult, [okey, "rden"], ["otb%d" % qi])
                C.dma(A["OT%d" % s][h, :, q0:q0 + nq], otb[qi][:, 0:nq], r=["otb%d" % qi], q=STQ)

        NB = len(blocks)
        for g0 in range(min(2, len(groups))):
            load_q(g0)
        for c0 in range(min(3, len(chunks))):
            load_chunk(c0)
        for i in range(NB + LA):
            if i < NB:
                emit_S(i)
            if i - LA >= 0:
                emit_PV(i - LA)
        C.P.build()

    if NPH < 5:
        return nc
    with contextlib.ExitStack() as st:
        C = Ctx(nc, st)
        C.load_consts(A)
        stg = [C.sb("stg%d" % i, [128, 2048], F32) for i in range(2)]
        womla = C.sb("womla", [128, 8, D], BF16)
        wout = C.sb("wout", [128, 8, D], BF16)
        C.load_weight("womla", womla, A["w_o_mla"], 8, D, stg=stg)
        C.load_weight("wout", wout, A["w_out"], 8, D, stg=stg)
        xt = [C.sb("xt%d" % i, [128, D], F32) for i in range(2)]
        sga = [C.sb("sga%d" % i, [128, D], F32) for i in range(2)]
        ybg = [C.sb("ybg%d" % i, [128, D], F32) for i in range(2)]
        ot = [C.sb("ot%d" % i, [128, H, 128], BF16) for i in range(2)]
        mixb = [C.sb("mixb%d" % i, [128, D], BF16) for i in range(2)]
        mixT = [C.sb("mixT%d" % i, [128, 8, 128], BF16) for i in range(2)]
        tiles = [(s, t) for s in range(NS) for t in range(TQ[s] // 128)]

        def p4a_loads(n):
            s, t = tiles[n]
            i = n % 2
            p = "%d" % i
            tok = slice(t * 128, (t + 1) * 128)
            C.dma(xt[i][:], A["xo%d" % s][tok, :], w=["xt" + p])
            C.dma(sga[i][:], A["siga%d" % s][tok, :], w=["sga" + p])
            C.dma(ybg[i][:], A["ybg%d" % s][tok, :], w=["ybg" + p])
            C.dma(ot[i][:], A["OT%d" % s][:, :, tok].rearrange("h v t -> v h t"), w=["ot" + p])

        p4a_loads(0)
        for n, (s, t) in enumerate(tiles):
            if True:
                if n + 1 < len(tiles):
                    p4a_loads(n + 1)
                i = n % 2
                p = "%d" % i
                tok = slice(t * 128, (t + 1) * 128)
                for g in range(2):
                    pb, pk = C.bank()
                    for h in range(H):
                        C.mm(pb[:, :], ot[i][:, h, :], womla[:, h, g * 512:(g + 1) * 512], h == 0, h == H - 1,
                             ["ot" + p, "womla"], [pk], inc=(h == H - 1))
                    C.tt(sga[i][:, g * 512:(g + 1) * 512], pb[:, :], sga[i][:, g * 512:(g + 1) * 512], ALU.mult,
                         [pk, "sga" + p], ["sga" + p])
                C.tt(mixb[i][:], sga[i][:], ybg[i][:], ALU.add, ["sga" + p, "ybg" + p], ["mixb" + p])
                pb, pk = C.bank()
                pvw = pb[:].bitcast(BF16)
                for c in range(8):
                    C.tr(pvw[:, c * 128:(c + 1) * 128], mixb[i][:, c * 128:(c + 1) * 128], C.ident[:], ["mixb" + p, "ident"],
                         [pk], inc=(c == 7))
                C.copy(mixT[i][:].rearrange("p c t -> p (c t)"), pvw[:, :], [pk], ["mixT" + p], eng="act")
                for g in range(2):
                    pb, pk = C.bank()
                    for c in range(8):
                        C.mm(pb[:, :], mixT[i][:, c, :], wout[:, c, g * 512:(g + 1) * 512], c == 0, c == 7,
                             ["mixT" + p, "wout"], [pk], inc=(c == 7))
                    C.tt(xt[i][:, g * 512:(g + 1) * 512], pb[:, :], xt[i][:, g * 512:(g + 1) * 512], ALU.add, [pk, "xt" + p],
                         ["xt" + p])
                C.dma(A["x1_%d" % s][tok, :], xt[i][:], r=["xt" + p], q=STQ)
        C.P.build()

    if NPH < 6:
        return nc
    with contextlib.ExitStack() as st:
        C = Ctx(nc, st)
        C.load_consts(A)
        stg = [C.sb("stg%d" % i, [128, 2048], F32) for i in range(2)]
        gains = C.sb("gains", [128, 8], F32)
        C.dma(gains[:, 0:8], A["ffn_norm"], w=["gains"])
        wg = C.sb("wg", [128, 8, DFF], BF16)
        wu = C.sb("wu", [128, 8, DFF], BF16)
        wd = C.sb("wd", [128, 22, D], BF16)
        C.load_weight("wg", wg, A["w_ffn_gate"], 8, DFF, gain=gains[:, 0:8], stg=stg)
        C.load_weight("wu", wu, A["w_ffn_up"], 8, DFF, gain=gains[:, 0:8], stg=stg)
        C.load_weight("wd", wd, A["w_ffn_down"], 22, D, stg=stg)
        C.junk = C.sb("junk", [128, D], F32)
        C.ssx = [C.sb("ssx%d" % i, [128, 4], F32) for i in range(2)]
        C.hb = [C.sb("hb%d" % i, [128, D], BF16) for i in range(2)]
        xt = [C.sb("xt%d" % i, [128, D], F32) for i in range(2)]
        h2T = [C.sb("h2T%d" % i, [128, 8, 128], BF16) for i in range(2)]
        sgl = [C.sb("sgl%d" % i, [128, 512], F32) for i in range(2)]
        actb = C.sb("actb", [128, DFF], BF16)
        actT = C.sb("actT", [128, 22, 128], BF16)
        yo = [C.sb("yo%d" % i, [128, D], F32) for i in range(2)]
        tiles = [(s, t) for s in range(NS) for t in range(TQ[s] // 128)]

        def p4b_loads(n):
            s, t = tiles[n]
            i = n % 2
            C.dma(xt[i][:], A["x1_%d" % s][t * 128:(t + 1) * 128, :], w=["xt%d" % i])

        p4b_loads(0)
        for n, (s, t) in enumerate(tiles):
            if True:
                if n + 1 < len(tiles):
                    p4b_loads(n + 1)
                i = n % 2
                p = "%d" % i
                tok = slice(t * 128, (t + 1) * 128)
                C.xnorm_T((xt[i][:], "xt" + p), 128, i, h2T[i][:], "h2T" + p)
                cg = 0
                for c0 in range(0, DFF, 512):
                    nn = min(512, DFF - c0)
                    pg, kg = C.bank()
                    pu, ku = C.bank()
                    for c in range(8):
                        C.mm(pg[:, 0:nn], h2T[i][:, c, :], wg[:, c, c0:c0 + nn], c == 0, c == 7, ["h2T" + p, "wg"], [kg],
                             inc=False)
                    for c in range(8):
                        C.mm(pu[:, 0:nn], h2T[i][:, c, :], wu[:, c, c0:c0 + nn], c == 0, c == 7, ["h2T" + p, "wu"], [ku],
                             inc=(c == 7))
                    sp_ = cg % 2
                    cg += 1
                    C.act(sgl[sp_][:, 0:nn], pg[:, 0:nn], AF.Silu, [kg], ["sgl%d" % sp_])
                    C.tt(actb[:, c0:c0 + nn], pu[:, 0:nn], sgl[sp_][:, 0:nn], ALU.mult, [ku, "sgl%d" % sp_], ["actb"])
                for c4 in range(0, 22, 8):
                    ncb = min(8, 22 - c4)
                    pb, pk = C.bank()
                    pvw = pb[:].bitcast(BF16)
                    for c in range(ncb):
                        C.tr(pvw[:, c * 128:(c + 1) * 128], actb[:, (c4 + c) * 128:(c4 + c + 1) * 128], C.ident[:],
                             ["actb", "ident"], [pk], inc=(c == ncb - 1))
                    C.copy(actT[:, c4:c4 + ncb, :].rearrange("p c t -> p (c t)"), pvw[:, 0:ncb * 128], [pk], ["actT"],
                           eng=("act" if (c4 // 8) % 2 == 0 else "dve"))
                for g in range(2):
                    pb, pk = C.bank()
                    for c in range(22):
                        C.mm(pb[:, :], actT[:, c, :], wd[:, c, g * 512:(g + 1) * 512], c == 0, c == 21, ["actT", "wd"], [pk],
                             inc=(c == 21))
                    C.tt(yo[i][:, g * 512:(g + 1) * 512], pb[:, :], xt[i][:, g * 512:(g + 1) * 512], ALU.add, [pk, "xt" + p],
                         ["yo" + p])
                C.dma(A["y%d" % s][tok, :], yo[i][:], r=["yo" + p], q=STQ)
        C.P.build()
    return nc


def _rope_tables(length):
    inv = (1.0 / (np.float32(10000.0) ** (np.arange(0, ROPE, 2, dtype=np.float32) / np.float32(ROPE)))).astype(np.float32)
    ang = (np.arange(length, dtype=np.float32)[:, None] * inv[None, :]).astype(np.float32)
    c = np.cos(ang).astype(np.float32)
    s = np.sin(ang).astype(np.float32)
    return np.concatenate([c, c], 1), np.concatenate([-s, s], 1)


def _consts():
    j = np.arange(128)[:, None]
    i = np.arange(128)[None, :]
    cm = np.zeros((128, 5, 128), np.float32)
    cm[:, 0] = (j > i)
    cm[:, 1] = (j <= i).astype(np.float32) - (j <= 63)
    cm[:, 2] = (j <= i)
    cm[:, 3] = (j >= i).astype(np.float32) - (j >= 64)
    cm[:, 4] = (j >= i)
    cr = np.zeros((128, 4), np.float32)
    cr[:, 0] = 1.0
    cr[:, 1] = (np.arange(128) <= 63)
    cr[:, 2] = 1.0
    cr[:, 3] = (np.arange(128) >= 64)
    return cm, cr


def make_in_maps(inp, TQ, n_groups, seq_arrays):
    f = lambda a: np.ascontiguousarray(np.asarray(a, dtype=np.float32))
    w_in = f(inp["w_in"][0])
    offs = np.cumsum([0, QL, KVL, ROPE, 512, 512, 1024, 1024, 16, 16, 1024, 1024])
    seg = lambda i: w_in[:, offs[i]:offs[i + 1]]
    w_ctx = np.concatenate([seg(1), seg(2), seg(4), seg(5)], 1)
    w_main = np.concatenate([seg(0), seg(3), seg(4), seg(5), seg(6), seg(7), seg(8), seg(9), seg(10)], 1)
    wa = [seg(7), seg(8)]
    wa2 = [np.concatenate([f(inp["w_a2_fwd"][0]), f(inp["b_a2_fwd"])], 0),
           np.concatenate([f(inp["w_a2_bwd"][0]), f(inp["b_a2_bwd"])], 0)]
    cm, cr = _consts()
    shared = {"meta": f(inp["meta_tokens"]), "w_ctx": f(w_ctx), "w_main": f(w_main), "w_uq": f(inp["w_uq"][0]),
              "w_ukv": f(inp["w_ukv"][0]), "w_o_mla": f(inp["w_o_mla"][0]), "w_o_gla": f(inp["w_o_gla"][0]),
              "w_out": f(inp["w_out"][0]), "w_ffn_gate": f(inp["w_ffn_gate"][0]), "w_ffn_up": f(inp["w_ffn_up"][0]),
              "w_ffn_down": f(inp["w_ffn_down"][0]), "attn_norm": f(np.asarray(inp["attn_norm"], np.float32).reshape(-1, 128).T), "q_a_norm": f(np.asarray(inp["q_a_norm"], np.float32).reshape(-1, 128).T),
              "kv_a_norm": f(np.asarray(inp["kv_a_norm"], np.float32).reshape(-1, 128).T), "q_norm": f(inp["q_norm"]), "k_norm": f(inp["k_norm"]),
              "gla_o_norm": f(np.asarray(inp["gla_o_norm"], np.float32).reshape(-1, 128).T), "ffn_norm": f(np.asarray(inp["ffn_norm"], np.float32).reshape(-1, 128).T), "cmat": cm, "crhs": cr,
              "wa2_m": np.stack(wa2, 0)}
    rt = [_rope_tables(NM + 4 * T) for T in TQ]
    shared["rcm"] = f(rt[0][0][:NM])
    shared["rsm"] = f(rt[0][1][:NM])
    maps = []
    for g in range(n_groups):
        for j in range(4):
            m = dict(shared)
            slots = [(qq, 0) for qq in range(j)] + [(qq, 1) for qq in range(3, j, -1)] + [(j, 1)]
            flags = np.zeros((5, 128, 2), np.float32)
            for sl, (qq, d) in enumerate(slots):
                flags[sl, :, d] = 1.0
            flags[4, :, 0] = 1.0
            m["flags"] = flags
            dirs = [d for (_, d) in slots] + [0]
            m["wa_c"] = f(np.stack([wa[d] for d in dirs], 0))
            m["wa2_c"] = f(np.stack([wa2[d] for d in dirs], 0))
            for s, T in enumerate(TQ):
                x = seq_arrays[s][g]
                rcf, rsf = rt[s]
                pos = lambda qq: np.arange(qq * T, (qq + 1) * T)
                m["xo%d" % s] = f(x[pos(j)])
                m["rco%d" % s] = f(rcf[NM + pos(j)])
                m["rso%d" % s] = f(rsf[NM + pos(j)])
                xs, cs, ss_ = [], [], []
                for (qq, d) in slots:
                    idx = pos(qq)[::-1] if d == 1 else pos(qq)
                    xs.append(x[idx])
                    cs.append(rcf[NM + idx])
                    ss_.append(rsf[NM + idx])
                m["xc%d" % s] = f(np.stack(xs, 0))
                m["rcc%d" % s] = f(np.stack(cs, 0))
                m["rsc%d" % s] = f(np.stack(ss_, 0))
            maps.append(m)
    return maps


_CACHE = {}


def run(inp, TQ, seq_arrays, n_groups):
    key = tuple(TQ)
    if key not in _CACHE:
        _CACHE[key] = build(list(TQ))
    nc = _CACHE[key]
    maps = make_in_maps(inp, TQ, n_groups, seq_arrays)
    res = run_bass_kernel_spmd(nc, maps, core_ids=list(range(len(maps))))
    outs = []
    for s, T in enumerate(TQ):
        y = np.zeros((n_groups, 4 * T, D), np.float32)
        for g in range(n_groups):
            for j in range(4):
                y[g, j * T:(j + 1) * T] = res.results[g * 4 + j]["y%d" % s]
        outs.append(y)
    return outs


def kernel(**inputs):
    xp = np.asarray(inputs["x_prompt"], dtype=np.float32)
    xs = np.asarray(inputs["x_sample"], dtype=np.float32)
    TQ = [xs.shape[1] // 4, xp.shape[1] // 4]
    ys, yp = run(inputs, TQ, [xs, xp], 2)
    return (yp, ys)
```
